# Optimizing a Trainium2 kernel written in Bass

```python
import math
import jax, jax.numpy as jnp
from jax import lax
import numpy as np

D_MODEL = 1024
BATCH = 8
SEQ = 2048
DEPTH = 1
DEC_BATCH = 128
DEC_SEQ = 4
PAST_LEN = 16384
PAGE_SIZE = 128

H_A = 4
DK_A = 128
DV_A = 128
QK_A = H_A * DK_A
VW_A = H_A * DV_A
QKV_W = 2 * QK_A + VW_A
CONV_A = 4
CHUNK_A = 64
H_B = 4
E_B = 128
DV_B = 128
HE_B = H_B * E_B
VW_B = H_B * DV_B
CHUNK_B = 32
D_FF = 2816
CONV_F = 3
EPS = 1e-6

IN_SPLITS = (QK_A, QK_A, VW_A, H_A, H_A, VW_A, HE_B, HE_B, VW_B, VW_B, D_MODEL, D_MODEL)
N_IN = sum(IN_SPLITS)

kernel_name = "gdn_hgrn2_convffn_hybrid_step"


def _rmsnorm(x, g):
    xf = x.astype(jnp.float32)
    y = xf * lax.rsqrt(jnp.mean(xf * xf, axis=-1, keepdims=True) + EPS) * g.astype(jnp.float32)
    return y.astype(x.dtype)


def _head_rmsnorm(o, g):
    return o * lax.rsqrt(jnp.mean(o * o, axis=-1, keepdims=True) + EPS) * g.astype(jnp.float32)


def _l2norm(x):
    return x * lax.rsqrt(jnp.sum(x * x, axis=-1, keepdims=True) + EPS)


def _causal_conv(x, buf, w):
    width = w.shape[0]
    L = x.shape[1]
    xp = jnp.concatenate([buf.astype(x.dtype), x], axis=1)
    y = xp[:, 0:L] * w[0]
    for j in range(1, width):
        y = y + xp[:, j:j + L] * w[j]
    return y, xp[:, xp.shape[1] - (width - 1):]


def _to_chunks(t, C):
    B, L, H, D = t.shape
    return t.reshape(B, L // C, C, H, D).transpose(1, 0, 3, 2, 4)


def _from_chunks(t):
    n, B, H, C, D = t.shape
    return t.transpose(1, 0, 3, 2, 4).reshape(B, n * C, H, D)


def _gated_delta(q, k, v, beta, g, S0):
    L = q.shape[1]
    C = math.gcd(L, CHUNK_A)
    q, k, v = _to_chunks(q, C), _to_chunks(k, C), _to_chunks(v, C)
    beta = _to_chunks(beta[..., None], C)[..., 0]
    G = jnp.cumsum(_to_chunks(g[..., None], C)[..., 0], axis=-1)
    idx = jnp.arange(C)
    causal = idx[:, None] >= idx[None, :]
    strict = idx[:, None] > idx[None, :]
    decay = jnp.exp(jnp.where(causal, G[..., :, None] - G[..., None, :], -jnp.inf))
    kb = k * beta[..., None]
    lmat = jnp.where(strict, jnp.einsum('nbhid,nbhjd->nbhij', kb, k) * decay, 0.0)
    eye = jnp.eye(C, dtype=q.dtype)
    a_mat = eye + lmat
    T = lax.linalg.triangular_solve(a_mat, jnp.broadcast_to(eye, a_mat.shape), left_side=True, lower=True)
    value = jnp.einsum('nbhij,nbhje->nbhie', T, v * beta[..., None])
    kcd = jnp.einsum('nbhij,nbhjd->nbhid', T, kb * jnp.exp(G)[..., None])
    attn = jnp.einsum('nbhid,nbhjd->nbhij', q, k) * decay
    G_last = G[..., -1]
    qg = q * jnp.exp(G)[..., None]
    kg = k * jnp.exp(G_last[..., None] - G)[..., None]

    def step(S, xs):
        value_n, kcd_n, attn_n, qg_n, kg_n, gl_n = xs
        u = value_n - jnp.einsum('bhcd,bhde->bhce', kcd_n, S)
        o = jnp.einsum('bhcd,bhde->bhce', qg_n, S) + jnp.einsum('bhij,bhje->bhie', attn_n, u)
        S = S * jnp.exp(gl_n)[..., None, None] + jnp.einsum('bhcd,bhce->bhde', kg_n, u)
        return S, o

    S, o = lax.scan(step, S0, (value, kcd, attn, qg, kg, G_last))
    return _from_chunks(o), S


def _hgrn2(q, k, v, logf, S0):
    L = q.shape[1]
    C = math.gcd(L, CHUNK_B)
    q, k, v = _to_chunks(q, C), _to_chunks(k, C), _to_chunks(v, C)
    Bc = jnp.cumsum(_to_chunks(logf, C), axis=-2)
    Bl = Bc[..., -1, :]
    qg = q * jnp.exp(Bc)
    kg = k * jnp.exp(Bl[..., None, :] - Bc)
    idx = jnp.arange(C)
    causal = (idx[:, None] >= idx[None, :])[:, :, None]

    def step(S, xs):
        q_n, k_n, v_n, b_n, qg_n, kg_n, bl_n = xs
        dec = jnp.exp(jnp.where(causal, b_n[..., :, None, :] - b_n[..., None, :, :], -jnp.inf))
        attn = jnp.einsum('bhie,bhje,bhije->bhij', q_n, k_n, dec)
        o = jnp.einsum('bhce,bhev->bhcv', qg_n, S) + jnp.einsum('bhij,bhjv->bhiv', attn, v_n)
        S = S * jnp.exp(bl_n)[..., None] + jnp.einsum('bhce,bhcv->bhev', kg_n, v_n)
        return S, o

    S, o = lax.scan(step, S0, (q, k, v, Bc, qg, kg, Bl))
    return _from_chunks(o), S


def _layer(x, conv_buf, s_delta, s_hgrn, ffn_buf, lb, g_attn, w_in, w_conv_a, a_log, dt_bias,
           g_out_a, w_branch_a, g_out_b, w_branch_b, w_out, g_ffn, w_ffn_gate, w_ffn_up,
           w_ffn_conv, w_ffn_down):
    f32 = jnp.float32
    dt = x.dtype
    B, L, _ = x.shape
    h = _rmsnorm(x, g_attn)
    z = h @ w_in
    offs = np.cumsum(IN_SPLITS)[:-1].tolist()
    qa, ka, va, aa, ba, oga, qb, fb, ib, ogb, gate_a, gate_b = jnp.split(z, offs, axis=-1)

    qkv, new_conv = _causal_conv(jnp.concatenate([qa, ka, va], axis=-1), conv_buf, w_conv_a)
    qkv = jax.nn.silu(qkv.astype(f32))
    q = qkv[..., :QK_A].reshape(B, L, H_A, DK_A)
    k = qkv[..., QK_A:2 * QK_A].reshape(B, L, H_A, DK_A)
    v = qkv[..., 2 * QK_A:].reshape(B, L, H_A, DV_A)
    q = _l2norm(q) * (DK_A ** -0.5)
    k = _l2norm(k)
    beta = jax.nn.sigmoid(ba.astype(f32))
    g = -jnp.exp(a_log.astype(f32)) * jax.nn.softplus(aa.astype(f32) + dt_bias.astype(f32))
    o_a, s_delta_new = _gated_delta(q, k, v, beta, g, s_delta.astype(f32))
    o_a = _head_rmsnorm(o_a, g_out_a) * jax.nn.silu(oga.astype(f32).reshape(B, L, H_A, DV_A))
    y_a = o_a.reshape(B, L, VW_A).astype(dt) @ w_branch_a

    qh = jax.nn.silu(qb.astype(f32)).reshape(B, L, H_B, E_B)
    f = lb + (1.0 - lb) * jax.nn.sigmoid(fb.astype(f32).reshape(B, L, H_B, E_B))
    o_b, s_hgrn_new = _hgrn2(qh, 1.0 - f, ib.astype(f32).reshape(B, L, H_B, DV_B), jnp.log(f), s_hgrn.astype(f32))
    o_b = _head_rmsnorm(o_b, g_out_b) * jax.nn.silu(ogb.astype(f32).reshape(B, L, H_B, DV_B))
    y_b = o_b.reshape(B, L, VW_B).astype(dt) @ w_branch_b

    mix = jax.nn.sigmoid(gate_a) * y_a + jax.nn.sigmoid(gate_b) * y_b
    x = x + (mix @ w_out).astype(dt)

    h2 = _rmsnorm(x, g_ffn)
    gc, new_ffn = _causal_conv(h2 @ w_ffn_gate, ffn_buf, w_ffn_conv)
    x = x + ((jax.nn.silu(gc) * (h2 @ w_ffn_up)) @ w_ffn_down).astype(dt)
    return x, new_conv, s_delta_new, s_hgrn_new, new_ffn


def setup_inputs(seed: int = 0) -> dict:
    key = jax.random.key(seed)
    ks = jax.random.split(key, 24)
    f32 = jnp.float32

    def nrm(k, shape, scale):
        return jax.random.normal(k, shape, f32) * scale

    dt_init = jnp.exp(jax.random.uniform(ks[10], (DEPTH, H_A), f32, math.log(1e-3), math.log(1e-1)))
    return {
        "x_prompt": nrm(ks[0], (BATCH, SEQ, D_MODEL), 1.0),
        "x_sample": nrm(ks[1], (DEC_BATCH, DEC_SEQ, D_MODEL), 1.0),
        "cache_conv_qkv": nrm(ks[2], (DEPTH, DEC_BATCH, CONV_A - 1, QKV_W), 1.0),
        "state_delta": nrm(ks[3], (DEPTH, DEC_BATCH, H_A, DK_A, DV_A), 0.05),
        "state_hgrn": nrm(ks[4], (DEPTH, DEC_BATCH, H_B, E_B, DV_B), 0.1),
        "cache_ffn_conv": nrm(ks[5], (DEPTH, DEC_BATCH, CONV_F - 1, D_FF), 1.0),
        "g_attn": 1.0 + nrm(ks[6], (DEPTH, D_MODEL), 0.02),
        "w_in": nrm(ks[7], (DEPTH, D_MODEL, N_IN), D_MODEL ** -0.5),
        "w_conv_a": nrm(ks[8], (DEPTH, CONV_A, QKV_W), CONV_A ** -0.5),
        "a_log": jnp.log(jax.random.uniform(ks[9], (DEPTH, H_A), f32, 1.0, 16.0)),
        "dt_bias": dt_init + jnp.log(-jnp.expm1(-dt_init)),
        "g_out_a": 1.0 + nrm(ks[11], (DEPTH, DV_A), 0.02),
        "w_branch_a": nrm(ks[12], (DEPTH, VW_A, D_MODEL), VW_A ** -0.5),
        "lb_logits": nrm(ks[13], (DEPTH + 1, HE_B), 0.5),
        "g_out_b": 1.0 + nrm(ks[14], (DEPTH, DV_B), 0.02),
        "w_branch_b": nrm(ks[15], (DEPTH, VW_B, D_MODEL), VW_B ** -0.5),
        "w_out": nrm(ks[16], (DEPTH, D_MODEL, D_MODEL), D_MODEL ** -0.5),
        "g_ffn": 1.0 + nrm(ks[17], (DEPTH, D_MODEL), 0.02),
        "w_ffn_gate": nrm(ks[18], (DEPTH, D_MODEL, D_FF), D_MODEL ** -0.5),
        "w_ffn_up": nrm(ks[19], (DEPTH, D_MODEL, D_FF), D_MODEL ** -0.5),
        "w_ffn_conv": nrm(ks[20], (DEPTH, CONV_F, D_FF), CONV_F ** -0.5),
        "w_ffn_down": nrm(ks[21], (DEPTH, D_FF, D_MODEL), D_FF ** -0.5),
        "g_final": 1.0 + nrm(ks[22], (D_MODEL,), 0.02),
    }


def reference(x_prompt, x_sample, cache_conv_qkv, state_delta, state_hgrn, cache_ffn_conv,
              g_attn, w_in, w_conv_a, a_log, dt_bias, g_out_a, w_branch_a, lb_logits,
              g_out_b, w_branch_b, w_out, g_ffn, w_ffn_gate, w_ffn_up, w_ffn_conv,
              w_ffn_down, g_final):
    f32 = jnp.float32
    lb_all = jnp.cumsum(jax.nn.softmax(lb_logits.astype(f32), axis=0), axis=0)
    xp, xs = x_prompt, x_sample
    Bp = x_prompt.shape[0]
    p_conv, p_delta, p_hgrn, p_ffn = [], [], [], []
    s_conv, s_delta, s_hgrn, s_ffn = [], [], [], []
    for l in range(DEPTH):
        lb = lb_all[l].reshape(H_B, E_B)
        lw = (g_attn[l], w_in[l], w_conv_a[l], a_log[l], dt_bias[l], g_out_a[l], w_branch_a[l],
              g_out_b[l], w_branch_b[l], w_out[l], g_ffn[l], w_ffn_gate[l], w_ffn_up[l],
              w_ffn_conv[l], w_ffn_down[l])
        xp, c1, d1, h1, f1 = _layer(
            xp, jnp.zeros((Bp, CONV_A - 1, QKV_W), xp.dtype), jnp.zeros((Bp, H_A, DK_A, DV_A), f32),
            jnp.zeros((Bp, H_B, E_B, DV_B), f32), jnp.zeros((Bp, CONV_F - 1, D_FF), xp.dtype), lb, *lw)
        xs, c2, d2, h2, f2 = _layer(
            xs, cache_conv_qkv[l], state_delta[l], state_hgrn[l], cache_ffn_conv[l], lb, *lw)
        p_conv.append(c1); p_delta.append(d1); p_hgrn.append(h1); p_ffn.append(f1)
        s_conv.append(c2); s_delta.append(d2); s_hgrn.append(h2); s_ffn.append(f2)
    y_prompt = _rmsnorm(xp, g_final)
    y_sample = _rmsnorm(xs, g_final)
    return (y_prompt, y_sample,
            jnp.stack(p_conv), jnp.stack(p_delta), jnp.stack(p_hgrn), jnp.stack(p_ffn),
            jnp.stack(s_conv), jnp.stack(s_delta), jnp.stack(s_hgrn), jnp.stack(s_ffn))
```

```python
import numpy as np
import concourse.bass as bass
import concourse.mybir as mybir

F32 = mybir.dt.float32
BF16 = mybir.dt.bfloat16
AF = mybir.ActivationFunctionType
ALU = mybir.AluOpType
SAME_ENG_SYNC = True
ATTACH_WAIT = True
TRANSITIVE = True
ATTACH_PE = True
NO_SAME_SYNC = ('pe', 'dve')


class T:
    def __init__(self, name, handle):
        self.name = name
        self.h = handle
        self.w = None
        self.r = {}

    def __getitem__(self, idx):
        return V(self, self.h.ap()[idx] if not isinstance(self.h, bass.AP) else self.h[idx])


class V:
    def __init__(self, tiles, ap):
        self.ts = tiles if isinstance(tiles, list) else [tiles]
        self.ap = ap

    def m(self, f):
        return V(self.ts, f(self.ap))

    def __getitem__(self, idx):
        return V(self.ts, self.ap[idx])


class Buf:
    def __init__(self, pages, ap):
        self.ts = pages
        self.ap = ap

    def __getitem__(self, idx):
        return V(self.ts, self.ap[idx])

    def m(self, f):
        return V(self.ts, f(self.ap))


PAGE = 1024


class Arena:
    def __init__(self, K, nbytes):
        self.K = K
        self.words = nbytes // 4
        self.h = K.nc.alloc_sbuf_tensor("arena", [128, self.words], F32)
        self.pages = [T("pg%d" % i, None) for i in range((nbytes + PAGE - 1) // PAGE)]
        self.top = 0
        self.peak = 0

    def alloc(self, dtype, shape):
        esz = 4 if dtype == F32 else 2
        nel = int(np.prod(shape[1:]))
        nb = nel * esz
        nbr = (nb + PAGE - 1) // PAGE * PAGE
        off = self.top
        self.top += nbr
        self.peak = max(self.peak, self.top)
        assert self.top <= self.words * 4, ("arena overflow", self.top)
        w0 = off // 4
        w1 = w0 + (nb + 3) // 4
        ap = self.h.ap()[0:shape[0], w0:w1]
        if dtype != F32:
            ap = ap.bitcast(dtype)
            ap = ap[:, 0:nel]
        if len(shape) == 3:
            ap = ap.rearrange("p (a b) -> p a b", a=shape[1])
        elif len(shape) == 4:
            ap = ap.rearrange("p (a b c) -> p a b c", a=shape[1], b=shape[2])
        return Buf(self.pages[off // PAGE:(off + nbr) // PAGE], ap)

    def mark(self):
        return self.top

    def release(self, m):
        self.top = m


def _ap(x):
    return x.ap if isinstance(x, V) else x


class Trk:
    def __init__(self, nc):
        self.nc = nc
        self.eng = {'pe': nc.tensor, 'act': nc.scalar, 'dve': nc.vector,
                    'pool': nc.gpsimd, 'sp': nc.sync}
        self.sem = {k: nc.alloc_semaphore('s_' + k) for k in self.eng}
        self.cnt = {k: 0 for k in self.eng}
        self.pend = {k: False for k in self.eng}
        self.seen = {k: {} for k in self.eng}
        self.dsem = {}
        self.ninst = {k: 0 for k in self.eng}
        self.tag = ''
        self.nwait = {k: 0 for k in self.eng}
        self.nattach = {k: 0 for k in self.eng}
        self.snap = {k: {} for k in self.eng}
        self.log = {k: [] for k in self.eng}

    def sb(self, name, shape, dtype):
        return T(name, self.nc.alloc_sbuf_tensor(name, list(shape), dtype))

    def ps(self, name, shape, dtype):
        return T(name, self.nc.alloc_psum_tensor(name, list(shape), dtype))

    def dsem_new(self, key):
        self.dsem[key] = [self.nc.alloc_semaphore('d_' + key), 0]

    def _waits(self, e, deps, defer=False):
        pending = []
        for key, val in deps.items():
            if key == e and (e in NO_SAME_SYNC or not SAME_ENG_SYNC):
                continue
            if key in self.dsem:
                val = self.dsem[key][1]
            if self.seen[e].get(key, 0) >= val:
                continue
            sem = self.sem[key] if key in self.sem else self.dsem[key][0]
            pending.append((sem, val))
            self.seen[e][key] = val
            if TRANSITIVE and key in self.snap:
                sn = self.snap[key].get(val)
                if sn:
                    for k2, v2 in sn.items():
                        if k2 != e and self.seen[e].get(k2, 0) < v2:
                            self.seen[e][k2] = v2
        last = None
        if defer and ATTACH_WAIT and pending:
            last = pending.pop()
        for sem, val in pending:
            self.eng[e].wait_ge(sem, val)
            self.nwait[e] += 1
        if last is not None:
            self.nattach[e] += 1
        return last

    @staticmethod
    def _add(deps, d):
        if d is None:
            return
        k, v = d
        if deps.get(k, 0) < v:
            deps[k] = v

    def _deps(self, reads, writes):
        deps = {}
        for t in reads:
            self._add(deps, t.w)
        for t in writes:
            self._add(deps, t.w)
            for k, v in t.r.items():
                self._add(deps, (k, v))
        return deps

    def op(self, e, fn, outs, ins, inc=True):
        reads = [t for x in ins if isinstance(x, V) for t in x.ts]
        writes = [t for x in outs if isinstance(x, V) for t in x.ts]
        last = self._waits(e, self._deps(reads, writes), defer=(e != 'pe' or ATTACH_PE))
        ins_ = fn()
        if last is not None:
            ins_._wait_ge(last[0], last[1])
        self.ninst[e] += 1
        self.log[e].append(self.tag)
        if inc:
            self.cnt[e] += 1
            ins_.then_inc(self.sem[e], 1)
            if TRANSITIVE:
                self.snap[e][self.cnt[e]] = dict(self.seen[e])
            me = (e, self.cnt[e])
            self.pend[e] = False
        else:
            me = (e, self.cnt[e] + 1)
            self.pend[e] = True
        for t in reads:
            if t.r.get(e, 0) < me[1]:
                t.r[e] = me[1]
        for t in writes:
            t.w = me
            t.r = {}
        return ins_

    def dma(self, q, out, in_, dkey):
        reads = list(in_.ts) if isinstance(in_, V) else []
        writes = list(out.ts) if isinstance(out, V) else []
        self._waits(q, self._deps(reads, writes))
        ins_ = self.eng[q].dma_start(out=_ap(out), in_=_ap(in_))
        ds = self.dsem[dkey]
        ds[1] += 16
        ins_.then_inc(ds[0], 16)
        me = (dkey, ds[1])
        for t in reads:
            t.r[dkey] = me[1]
        for t in writes:
            t.w = me
            t.r = {}
        return ins_

    def finish(self):
        for key, (sem, val) in self.dsem.items():
            if val > 0 and self.seen['sp'].get(key, 0) < val:
                self.eng['sp'].wait_ge(sem, val)
        for e in ('pe', 'act', 'dve', 'pool'):
            assert not self.pend[e], e
            if self.cnt[e] > 0:
                self.eng['sp'].wait_ge(self.sem[e], self.cnt[e])

    def mm(self, out, lhsT, rhs, start, stop, inc=None):
        if inc is None:
            inc = stop
        return self.op('pe', lambda: self.nc.tensor.matmul(_ap(out), _ap(lhsT), _ap(rhs), start=start, stop=stop),
                       [out], [lhsT, rhs], inc=inc)

    def tr(self, out, in_, ident, inc=True):
        return self.op('pe', lambda: self.nc.tensor.transpose(_ap(out), _ap(in_), _ap(ident)),
                       [out], [in_, ident], inc=inc)

    def act(self, out, in_, func, scale=1.0, bias=0.0, accum=None):
        kw = {}
        if accum is not None:
            kw['accum_out'] = _ap(accum)
        outs = [out] + ([accum] if accum is not None else [])
        ins = [in_] + [x for x in (scale, bias) if isinstance(x, V)]
        return self.op('act', lambda: self.nc.scalar.activation(_ap(out), _ap(in_), func, bias=_ap(bias),
                                                                scale=_ap(scale), **kw), outs, ins)

    def _ve(self, eng):
        return self.nc.vector if eng == 'dve' else self.nc.gpsimd

    def dve_copy(self, out, in_, eng='dve'):
        return self.op(eng, lambda: self._ve(eng).tensor_copy(_ap(out), _ap(in_)), [out], [in_])

    def tt(self, out, in0, in1, op, eng='dve'):
        return self.op(eng, lambda: self._ve(eng).tensor_tensor(_ap(out), _ap(in0), _ap(in1), op), [out], [in0, in1])

    def ts(self, out, in0, s1, s2, op0, op1=None, eng='dve', accum=None):
        ins = [in0] + [x for x in (s1, s2) if isinstance(x, V)]
        kw = {}
        outs = [out]
        if accum is not None:
            kw['accum_out'] = _ap(accum)
            outs.append(accum)
        if op1 is None:
            return self.op(eng, lambda: self._ve(eng).tensor_scalar(_ap(out), _ap(in0), _ap(s1), None, op0, **kw), outs, ins)
        return self.op(eng, lambda: self._ve(eng).tensor_scalar(_ap(out), _ap(in0), _ap(s1), _ap(s2), op0, op1, **kw), outs, ins)

    def stt(self, out, in0, scalar, in1, op0, op1):
        ins = [in0, in1] + ([scalar] if isinstance(scalar, V) else [])
        return self.op('dve', lambda: self.nc.vector.scalar_tensor_tensor(_ap(out), _ap(in0), _ap(scalar), _ap(in1), op0, op1),
                       [out], ins)

    def memset(self, out, val, eng='dve'):
        return self.op(eng, lambda: self._ve(eng).memset(_ap(out), val), [out], [])
from concourse.bass_utils import run_bass_kernel_spmd

D = 1024
HG_IN_DIN = True
NIN = 6152
DFF = 2816
NFC = 22
EPS = 1e-6
OFF = dict(qa=0, ka=512, va=1024, aa=1536, oga=1544, qb=2056, fb=2568, ib=3080, ogb=3592, ga=4104, gb=5128)
NCORES = 8
WSLOT_EL = 8 * 520


def _masks(grp, pos):
    n = len(grp)
    same = grp[:, None] == grp[None, :]
    le = pos[:, None] <= pos[None, :]
    gt = pos[:, None] > pos[None, :]
    tri = (same & le).astype(np.float32)
    gtm = (same & gt).astype(np.float32)
    mst = (same & (pos[None, :] > pos[:, None])).astype(np.float32)
    mit = (same & (pos[None, :] >= pos[:, None])).astype(np.float32)
    ng = int(grp.max()) + 1
    sel = (grp[:, None] == np.arange(ng)[None, :]).astype(np.float32)

    def pad(a):
        o = np.zeros((128,) + a.shape[1:], np.float32)
        o[:a.shape[0]] = a
        return o

    def rep4(a):
        return pad(np.repeat(a[:, None, :], 4, axis=1))
    selp = np.zeros((128, 16), np.float32)
    selp[:n, :ng] = sel
    out = np.zeros((128, 128 * 2 + 512 * 3 + 16), np.float32)
    out[:, 0:n] = pad(tri)
    out[:, 128:128 + n] = pad(gtm)
    w = n
    out[:, 256:256 + 4 * w] = rep4(mst).reshape(128, -1)
    out[:, 768:768 + 4 * w] = rep4(mit).reshape(128, -1)
    out[:, 1280:1280 + 4 * w] = rep4(gtm).reshape(128, -1)
    out[:, 1792:1808] = selp
    return out


def _consts():
    i = np.arange(128)
    mp = _masks(i // 64, i % 64)
    j = np.arange(64)
    ms = _masks(j % 16, j // 16)
    ident = np.eye(128, dtype=np.float32)
    seqcol = np.zeros((128, 16, 64), np.float32)
    for s in range(16):
        seqcol[:, s, s::16] = 1.0
    return dict(mask_p=mp, mask_s=ms, ident=ident, seqcol=seqcol.reshape(128, 1024))


_NC_CACHE = {}


def build_nc():
    nc = bass.Bass("TRN2", target_bir_lowering=False)
    K = Trk(nc)

    def din(name, shape):
        return nc.dram_tensor(name, list(shape), F32, kind="ExternalInput").ap()

    def dout(name, shape):
        return nc.dram_tensor(name, list(shape), F32, kind="ExternalOutput").ap()

    xp_d = din("xp", [2048, D]); xs_d = din("xs", [64, D])
    cq_d = din("cq", [128, 12, 48]); cf_d = din("cf", [128, NFC, 32])
    sd_d = din("sd", [16, 4, 128, 128]); sh_d = din("sh", [16, 4, 128, 128])
    g_attn_d = din("g_attn", [1, D]); g_ffn_d = din("g_ffn", [1, D]); g_fin_d = din("g_final", [1, D])
    w_in_d = din("w_in", [D, NIN]); wca_d = din("w_conv_a", [128, 12, 4]); wcf_d = din("w_ffn_conv", [128, NFC, 3])
    alog_d = din("a_log", [1, 4]); dtb_d = din("dt_bias", [1, 4])
    goa_d = din("g_out_a", [128, 1]); gob_d = din("g_out_b", [128, 1])
    wba_d = din("w_branch_a", [512, D]); wbb_d = din("w_branch_b", [512, D]); wo_d = din("w_out", [D, D])
    wg_d = din("w_ffn_gate", [D, DFF]); wu_d = din("w_ffn_up", [D, DFF]); wd_d = din("w_ffn_down", [DFF, D])
    lbt_d = din("lb_t", [2, 512]); lbf_d = din("lb_f", [128, 2, 4])
    mp_d = din("mask_p", [128, 1808]); ms_d = din("mask_s", [128, 1808])
    id_d = din("ident", [128, 128]); sc_d = din("seqcol", [128, 1024])

    yp_d = dout("yp", [2048, D]); ys_d = dout("ys", [64, D])
    ocq_p = dout("ocq_p", [128, 12, 3]); ocq_s = dout("ocq_s", [128, 12, 48])
    off_p = dout("off_p", [128, NFC, 2]); off_s = dout("off_s", [128, NFC, 32])
    od_p = dout("od_p", [4, 128, 128]); od_s = dout("od_s", [16, 4, 128, 128])
    oh_p = dout("oh_p", [4, 128, 128]); oh_s = dout("oh_s", [16, 4, 128, 128])

    A = Arena(K, 206 * 1024)
    for k in ("c", "cp", "x", "st", "sty0", "sty1", "ss", "sq", "x0", "x1", "x2", "x3"):
        K.dsem_new(k)
    NSLOT = 5
    for i in range(NSLOT + 8):
        K.dsem_new("w%d" % i)
        K.dsem_new("wst%d" % i)

    psum_h = nc.alloc_psum_tensor("psum_all", [128, 8 * 512], F32)
    bank_t = [T("bank%d" % i, None) for i in range(8)]

    class Bank:
        def __init__(self, i, n=1, dtype=F32):
            self.i = i
            self.n = n
            self.dtype = dtype

        def _ap(self):
            ap = psum_h.ap()[:, self.i * 512:(self.i + self.n) * 512]
            if self.dtype != F32:
                ap = ap.bitcast(self.dtype)
            return ap

        def __getitem__(self, idx):
            return V(bank_t[self.i:self.i + self.n], self._ap()[idx])
    PA0, PA1, PM, PC, PC2, PU, PO = [Bank(i) for i in range(7)]
    PT = Bank(7, 1, BF16)
    PT32 = Bank(7)
    P4 = Bank(3, 4)
    ALLB = [PA0, PA1, PM, PC, PC2, PU, PO, PT32]

    IDF = A.alloc(F32, [128, 128]); IDB = A.alloc(BF16, [128, 128]); ONESB = A.alloc(BF16, [128, 128])
    ONEF = A.alloc(F32, [128, 4, 128]); I4 = A.alloc(F32, [128, 4, 128])
    MK = A.alloc(F32, [128, 1808])
    GATT = A.alloc(F32, [128, D]); GFFN = A.alloc(F32, [128, D]); GFIN = A.alloc(F32, [128, D])
    OMLT = A.alloc(F32, [128, 512]); OMLF = A.alloc(F32, [128, 4])
    SMALL = A.alloc(F32, [128, 64])
    WCA = A.alloc(F32, [128, 12, 4]); WCF = A.alloc(F32, [128, NFC, 3])
    CTP = A.alloc(F32, [128, 12, 3]); FTP = A.alloc(F32, [128, NFC, 2])
    SD32 = A.alloc(F32, [128, 4, 128]); SH32 = A.alloc(F32, [128, 4, 128])
    SDB = A.alloc(BF16, [128, 4, 128]); SHB = A.alloc(BF16, [128, 4, 128])
    WS = [A.alloc(BF16, [128, WSLOT_EL]) for _ in range(NSLOT)]
    wstate = {"i": 0}

    LOOK = 2
    plist = []

    wscr = {}

    def _emit_piece(n):
        dram_ap, shape = plist[n]
        nel = int(np.prod(shape[1:]))
        NPP = len(plist) // 5
        if n >= 4 * NPP + SW0 and WS2:
            ring = WS + WS2
            i = (n - (4 * NPP + SW0)) % len(ring)
            v2 = ring[i][:, 0:nel]
        else:
            i = n % NSLOT
            v2 = WS[i][:, 0:nel]
        v = v2
        if len(shape) == 3:
            v = v2.m(lambda ap: ap.rearrange("p (a b) -> p a b", a=shape[1]))
        NPP = len(plist) // 5
        j = n % NPP
        if n < NPP:
            K.dma('pool', v, dram_ap, "w%d" % i)
            sc = nc.dram_tensor("wsc%d" % j, [128, nel], BF16).ap()
            wscr[j] = V(T("wsc%d" % j, None), sc)
            K.dma('sp', wscr[j], v2, "wst%d" % i)
        else:
            K.dma('pool', v2, wscr[j], "w%d" % i)
        return v

    wviews = {}
    WS2 = []
    NS2 = 8
    SW0 = 7

    def wpiece(dram_ap, shape):
        n = wstate["i"]
        wstate["i"] += 1
        assert plist[n][1] == shape, (n, plist[n][1], shape)
        look = LOOK
        sw = 4 * (len(plist) // 5) + SW0
        if WS2 and n >= sw:
            look = NSLOT + NS2 - 3
        for m in range(n, min(n + look + 1, len(plist))):
            if NS2 > 0 and m >= sw and not WS2:
                break
            if m not in wviews:
                wviews[m] = _emit_piece(m)
        return wviews[m if False else n]

    K.dma('sp', IDF[:, :], id_d, 'c')
    K.dma('pool', IDB[:, :], id_d, 'cp')
    K.memset(ONESB[:, :], 1.0)
    K.memset(ONEF[:, :, :], 1.0)
    for h in range(4):
        K.dma('sp', I4[:, h, :], id_d, 'c')
    K.dma('sp', GATT[:, :], g_attn_d.partition_broadcast(128), 'c')
    K.dma('sp', GFFN[:, :], g_ffn_d.partition_broadcast(128), 'c')
    K.dma('sp', GFIN[:, :], g_fin_d.partition_broadcast(128), 'c')
    K.dma('sp', WCA[:, :, :], wca_d, 'c')
    K.dma('sp', WCF[:, :, :], wcf_d, 'c')
    K.dma('sp', SMALL[:, 0:4], dtb_d.partition_broadcast(128), 'c')
    K.dma('sp', SMALL[:, 4:8], alog_d.partition_broadcast(128), 'c')
    K.dma('sp', SMALL[:, 8:9], goa_d, 'c')
    K.dma('sp', SMALL[:, 9:10], gob_d, 'c')
    m0 = A.mark()
    LBT = A.alloc(F32, [128, 2, 512]); LBF = A.alloc(F32, [128, 2, 4])
    K.dma('sp', LBT[:, :, :], lbt_d.partition_broadcast(128), 'c')
    K.dma('sp', LBF[:, :, :], lbf_d, 'c')
    K.tt(LBT[:, 1, :], LBT[:, 1, :], LBT[:, 0, :], ALU.subtract)
    K.act(OMLT[:, :], LBT[:, 1, :], AF.Sigmoid)
    K.tt(LBF[:, 1, :], LBF[:, 1, :], LBF[:, 0, :], ALU.subtract)
    K.act(OMLF[:, :], LBF[:, 1, :], AF.Sigmoid)
    K.act(SMALL[:, 4:8], SMALL[:, 4:8], AF.Exp)
    K.ts(SMALL[:, 4:8], SMALL[:, 4:8], -1.0, None, ALU.mult)
    A.release(m0)
    K.memset(CTP[:, :, :], 0.0); K.memset(FTP[:, :, :], 0.0)
    K.memset(SD32[:, :, :], 0.0); K.memset(SH32[:, :, :], 0.0)
    K.memset(SDB[:, :, :], 0.0); K.memset(SHB[:, :, :], 0.0)
    DTB = SMALL[:, 0:4]; NEGA = SMALL[:, 4:8]; GOA = SMALL[:, 8:9]; GOB = SMALL[:, 9:10]

    w_in_v = w_in_d.rearrange("(k p) n -> p k n", p=128)
    wg_v = wg_d.rearrange("(k p) n -> p k n", p=128)
    wu_v = wu_d.rearrange("(k p) n -> p k n", p=128)
    wo_v = wo_d.rearrange("(k p) n -> p k n", p=128)
    wba_v = wba_d.rearrange("(k p) n -> p k n", p=128)
    wbb_v = wbb_d.rearrange("(k p) n -> p k n", p=128)

    FILL = {"n": 0}

    def filler(k=None):
        k = FILL["n"] if k is None else k
        for _ in range(k):
            nc.tensor.matmul(psum_h.ap()[:, 2 * 512 + 64:2 * 512 + 192], IDB[:, :].ap, IDB[:, :].ap, start=True, stop=True)

    def bc3(v, shape, axis):
        return v.m(lambda ap: ap.unsqueeze(axis).to_broadcast(shape))

    PREF = {}

    def run_pass(kind, t0):
        sample = kind == 'S'
        TT = 64 if sample else 512
        BW = 64 if sample else 128
        NB = TT // BW
        sh = 16 if sample else 1
        HQ = 3 * sh
        HF = 2 * sh
        NG = 16 if sample else 2
        chunks = [(0, 64)] if sample else [(0, 64), (64, 128)]
        x_src = xs_d if sample else xp_d[t0:t0 + TT, :]
        y_dst = ys_d if sample else yp_d[t0:t0 + TT, :]
        K.dma('sp', MK[:, :], ms_d if sample else mp_d, 'c')
        TRI = MK[0:BW, 0:BW]; GTM = MK[0:BW, 128:128 + BW]

        def m4(c0):
            return MK[0:BW, c0:c0 + 4 * BW].m(lambda ap: ap.rearrange("p (h i) -> p h i", h=4))
        MST4 = m4(256); MIT4 = m4(768); MS4 = m4(1280)
        SEL = MK[0:BW, 1792:1792 + NG]
        mp_ = A.mark()
        X = [A.alloc(F32, [BW, D]) for _ in range(NB)]
        HT = A.alloc(BF16, [128, 8, TT])
        HN = A.alloc(BF16, [BW, D]); JUNK = HN; ST = A.alloc(F32, [128, 8])
        OA = A.alloc(F32, [128, 4, TT]); OB = A.alloc(F32, [128, 4, TT])
        SOA = A.alloc(BF16, [128, 4, TT])
        if sample:
            CT = A.alloc(F32, [128, 12, HQ]); FT = A.alloc(F32, [128, NFC, HF])
            K.dma('sp', CT[:, :, :], cq_d, 'x'); K.dma('sp', FT[:, :, :], cf_d, 'x')
            SEQC = A.alloc(F32, [128, 16, 64])
            K.dma('sp', SEQC[:, :, :], sc_d.rearrange("p (s i) -> p s i", s=16), 'x')
        else:
            CT = CTP; FT = FTP
        if not PREF.get('done'):
            for b in range(NB):
                K.dma('sp', X[b][:, :], x_src[b * BW:(b + 1) * BW, :], 'x%d' % b)
        PREF['done'] = False

        JB = [Bank(0, 2), Bank(0, 2)]
        TB = [PT, Bank(6, 1, BF16)]

        def norm_to_HT(grow):
            for b in range(NB):
                so = (b % 2) * 4
                K.act(JB[b % 2][0:BW, :], X[b][:, :], AF.Square, accum=ST[0:BW, so:so + 1])
                K.act(ST[0:BW, so + 1:so + 2], ST[0:BW, so:so + 1], AF.Ln, scale=1.0 / D, bias=EPS)
                K.act(ST[0:BW, so + 2:so + 3], ST[0:BW, so + 1:so + 2], AF.Exp, scale=-0.5)
                K.stt(HN[:, :], X[b][:, :], ST[0:BW, so + 2:so + 3], grow[0:BW, :], ALU.mult, ALU.mult)
                tb = TB[b % 2]
                for k in range(8):
                    K.tr(tb[:, k * BW:(k + 1) * BW], HN[:, k * 128:(k + 1) * 128], IDB[0:BW, 0:BW], inc=(k == 7))
                K.dve_copy(HT[:, :, b * BW:(b + 1) * BW],
                           tb[:, 0:8 * BW].m(lambda ap: ap.rearrange("p (k t) -> p k t", k=8)))

        def norm_pre(b, grow):
            so = (b % 2) * 4
            K.act(JB[b % 2][0:BW, :], X[b][:, :], AF.Square, accum=ST[0:BW, so:so + 1])
            K.act(ST[0:BW, so + 1:so + 2], ST[0:BW, so:so + 1], AF.Ln, scale=1.0 / D, bias=EPS)
            K.act(ST[0:BW, so + 2:so + 3], ST[0:BW, so + 1:so + 2], AF.Exp, scale=-0.5)
            K.stt(HN[:, :], X[b][:, :], ST[0:BW, so + 2:so + 3], grow[0:BW, :], ALU.mult, ALU.mult)

        def norm_post(b):
            tb = TB[b % 2]
            for k in range(8):
                K.tr(tb[:, k * BW:(k + 1) * BW], HN[:, k * 128:(k + 1) * 128], IDB[0:BW, 0:BW], inc=(k == 7))
            K.dve_copy(HT[:, :, b * BW:(b + 1) * BW],
                       tb[:, 0:8 * BW].m(lambda ap: ap.rearrange("p (k t) -> p k t", k=8)))

        pa = {"i": 0}

        def nextbank():
            bl = pa.get("banks", (PA0, PA1))
            P = bl[pa["i"] % len(bl)]
            pa["i"] += 1
            return P

        def fm_chunk(wv, c0):
            bl = pa.get("banks", (PA0, PA1))
            P = bl[pa["i"] % len(bl)]
            pa["i"] += 1
            for k in range(8):
                K.mm(P[:, 0:TT], wv[:, k, c0:c0 + 128], HT[:, k, :], k == 0, k == 7)
            return P[:, 0:TT]

        def rsq_fm(dst, src_ps, scale):
            K.act(dst, src_ps, AF.Ln, scale=scale, bias=EPS)
            K.act(dst, dst, AF.Exp, scale=-0.5)

        K.tag = kind + ':norm1'
        norm_to_HT(GATT)

        K.tag = kind + ':hg_in'
        mh = A.mark()
        VTOK = [A.alloc(BF16, [BW, 4, 128]) for _ in range(NB)]
        KG = [A.alloc(BF16, [BW, 4, 128]) for _ in range(NB)]
        QGB = A.alloc(BF16, [128, 4, TT])
        ATB = [A.alloc(BF16, [BW, 4, BW]) for _ in range(NB)]
        EBLC = A.alloc(F32, [128, 4, 2 * NB])
        mh2 = A.mark()
        SQB = A.alloc(F32, [128, 4, TT]); SGF = A.alloc(F32, [128, 4, TT])
        KTOK = [A.alloc(F32, [BW, 512]) for _ in range(NB)]
        LOGF = [A.alloc(F32, [BW, 512]) for _ in range(NB)]
        EBC = A.alloc(F32, [128, 4, TT]); ENB = A.alloc(F32, [128, 4, BW])
        KIB = A.alloc(BF16, [128, 4, TT])
        TMPT = A.alloc(F32, [BW, 512])
        wq = wpiece(w_in_v[:, :, OFF['qb']:OFF['qb'] + 512], [128, 8, 512])
        for h in range(4):
            K.act(SQB[:, h, :], fm_chunk(wq, h * 128), AF.Silu)
        wf = wpiece(w_in_v[:, :, OFF['fb']:OFF['fb'] + 512], [128, 8, 512])
        for h in range(4):
            K.act(SGF[:, h, :], fm_chunk(wf, h * 128), AF.Sigmoid, scale=-1.0)
        TB2 = (PM, PC2)
        for b in range(NB):
            P = TB2[b % 2]
            for k in range(8):
                K.mm(P[0:BW, :], HT[:, k, b * BW:(b + 1) * BW], wf[:, k, :], k == 0, k == 7)
            K.act(KTOK[b][:, :], P[0:BW, :], AF.Sigmoid, scale=-1.0)
            K.tt(KTOK[b][:, :], KTOK[b][:, :], OMLT[0:BW, :], ALU.mult)
        wi = wpiece(w_in_v[:, :, OFF['ib']:OFF['ib'] + 512], [128, 8, 512])
        for b in range(NB):
            P = TB2[b % 2]
            for k in range(8):
                K.mm(P[0:BW, :], HT[:, k, b * BW:(b + 1) * BW], wi[:, k, :], k == 0, k == 7)
            K.act(VTOK[b][:, :, :].m(lambda ap: ap.rearrange("p h v -> p (h v)")), P[0:BW, :], AF.Copy)
        for b in range(NB):
            K.act(LOGF[b][:, :], KTOK[b][:, :], AF.Ln, scale=-1.0, bias=1.0)
        K.tag = kind + ':hg_blk'
        for b in range(NB):
            bs = slice(b * BW, (b + 1) * BW)
            PCv = PC[:, 0:4 * BW].m(lambda ap: ap.rearrange("p (h i) -> p h i", h=4))
            for h in range(4):
                K.mm(PC[:, h * BW:(h + 1) * BW], LOGF[b][:, h * 128:(h + 1) * 128], TRI, True, True)
            K.act(EBC[:, :, bs], PCv, AF.Exp)
            K.act(ENB[:, :, :], PCv, AF.Exp, scale=-1.0)
            K.tt(QGB[:, :, bs], SQB[:, :, bs], EBC[:, :, bs], ALU.mult)
            for h in range(4):
                K.stt(KIB[:, h, bs], SGF[:, h, bs], OMLF[:, h:h + 1], ENB[:, h, :], ALU.mult, ALU.mult)
            K.mm(PM[0:BW, :], GTM, LOGF[b][:, :], True, True)
            K.act(TMPT[:, :], PM[0:BW, :], AF.Exp)
            K.tt(KG[b][:, :, :].m(lambda ap: ap.rearrange("p h v -> p (h v)")), KTOK[b][:, :], TMPT[:, :], ALU.mult)
            PUv = PU[0:BW, 0:4 * BW].m(lambda ap: ap.rearrange("p (h i) -> p h i", h=4))
            for h in range(4):
                K.mm(PU[0:BW, h * BW:(h + 1) * BW], KIB[:, h, bs], QGB[:, h, bs], True, True)
            K.tt(ATB[b][:, :, :], PUv, MIT4, ALU.mult)
        K.tag = kind + ':hg_ser'
        hg_gen = None
        if not sample:
            K.dve_copy(EBLC[:, :, :], EBC[:, :, :].m(lambda ap: ap.rearrange("p h (c t) -> p h c t", t=64))[:, :, :, 63])
            A.release(mh2)

            def hg_serial():
                for b in range(NB):
                    for ci, (r0, r1) in enumerate(chunks):
                        K.tag = kind + ':hg_ser'
                        c0 = b * BW + r0; c1 = b * BW + r1
                        for h in range(4):
                            K.mm(PA1[:, h * BW + r0:h * BW + r1], SHB[:, h, :], QGB[:, h, c0:c1], True, False)
                            K.mm(PA1[:, h * BW + r0:h * BW + r1], VTOK[b][r0:r1, h, :], ATB[b][r0:r1, h, r0:r1], False, True)
                        for h in range(4):
                            K.mm(PA0[:, h * 128:(h + 1) * 128], KG[b][r0:r1, h, :], VTOK[b][r0:r1, h, :], True, True)
                        for h in range(4):
                            K.stt(SH32[:, h, :], SH32[:, h, :], EBLC[:, h, b * 2 + ci:b * 2 + ci + 1],
                                  PA0[:, h * 128:(h + 1) * 128], ALU.mult, ALU.add)
                        K.act(SHB[:, :, :], SH32[:, :, :], AF.Copy)
                        filler()
                        yield
                    K.tag = kind + ':hg_ser'
                    K.act(OB[:, :, b * BW:(b + 1) * BW],
                          PA1[:, 0:4 * BW].m(lambda ap: ap.rearrange("p (h i) -> p h i", h=4)), AF.Copy)
                    yield
            hg_gen = hg_serial()
        else:
            ms_ = A.mark()
            S32 = A.alloc(F32, [128, 64, 128]); SBF = A.alloc(BF16, [128, 64, 128])
            QM = A.alloc(BF16, [128, 16, 64]); VM = A.alloc(BF16, [64, 16, 128]); EBL = A.alloc(F32, [128, 4, 16])
            K.dma('sp', S32[:, :, :], sh_d.rearrange("s h k v -> k (s h) v"), 'ss')
            K.dma('pool', SBF[:, :, :], sh_d.rearrange("s h k v -> k (s h) v"), 'sq')
            for h in range(4):
                K.mm(PM[:, h * 16:(h + 1) * 16], LOGF[0][:, h * 128:(h + 1) * 128], SEL, True, True)
            K.act(EBL[:, :, :], PM[:, 0:64].m(lambda ap: ap.rearrange("p (h s) -> p h s", h=4)), AF.Exp)
            for h in range(4):
                K.tt(QM[:, :, :], bc3(QGB[:, h, :], [128, 16, 64], 1), SEQC[:, :, :], ALU.mult, eng='pool')
                for s in range(16):
                    K.mm(PO[:, h * 64:(h + 1) * 64], SBF[:, s * 4 + h, :], QM[:, s, :], s == 0, False)
                K.mm(PO[:, h * 64:(h + 1) * 64], VTOK[0][:, h, :], ATB[0][:, h, :], False, True)
                K.tt(VM[:, :, :], bc3(VTOK[0][:, h, :], [64, 16, 128], 1),
                     bc3(SEL, [64, 16, 128], 2), ALU.mult)
                for s4 in range(4):
                    PB_ = (PC2, PC)[s4 % 2]
                    for q in range(4):
                        s = s4 * 4 + q
                        K.mm(PB_[:, q * 128:(q + 1) * 128], KG[0][:, h, :], VM[:, s, :], True, True)
                    for q in range(4):
                        s = s4 * 4 + q
                        K.stt(S32[:, s * 4 + h, :], S32[:, s * 4 + h, :], EBL[:, h, s:s + 1],
                              PB_[:, q * 128:(q + 1) * 128], ALU.mult, ALU.add)
            K.act(OB[:, :, :], PO[:, 0:256].m(lambda ap: ap.rearrange("p (h i) -> p h i", h=4)), AF.Copy)
            K.dma('sp', oh_s.rearrange("s h k v -> k (s h) v"), S32[:, :, :], 'st')
            A.release(ms_)
            A.release(mh)

        K.tag = kind + ':d_in'
        pa['banks'] = (PT32, PC, PC2, PU, PO)

        def hgs():
            if hg_gen is not None and HG_IN_DIN:
                try:
                    next(hg_gen)
                except StopIteration:
                    pass
            K.tag = kind + ':d_in'
        md = A.mark()
        QT = A.alloc(BF16, [128, 4, TT]); KT = A.alloc(BF16, [128, 4, TT]); VT = A.alloc(BF16, [128, 4, TT])
        GA = A.alloc(F32, [BW, NB, 4]); GG = A.alloc(F32, [BW, NB, 4]); BETA = A.alloc(F32, [BW, NB, 4])
        mdin = A.mark()
        ZB4 = [A.alloc(F32, [128, 4, HQ + TT]) for _ in range(1)]
        YC4 = [A.alloc(F32, [128, 4, TT]) for _ in range(2)]
        SQ4 = A.alloc(BF16, [128, 4, TT]); RS4 = A.alloc(F32, [128, 4, TT])
        fl = lambda v: v.m(lambda ap: ap.rearrange("p h t -> p (h t)"))
        PIECES = (('qa', QT), ('ka', KT), ('va', VT))

        def d_head(pi):
            nm, dst = PIECES[pi]
            wv = wpiece(w_in_v[:, :, OFF[nm]:OFF[nm] + 512], [128, 8, 512])
            base = {'qa': 0, 'ka': 4, 'va': 8}[nm]
            zb = ZB4[0]; yc = YC4[pi % 2]
            for h in range(4):
                ps = fm_chunk(wv, h * 128)
                ci = base + h
                K.dve_copy(zb[:, h, 0:HQ], CT[:, ci, :], eng='pool')
                K.act(zb[:, h, HQ:HQ + TT], ps, AF.Copy)
                K.dve_copy(CT[:, ci, :], zb[:, h, TT:TT + HQ], eng='pool')
                K.act(yc[:, h, :], zb[:, h, 0:TT], AF.Copy, scale=WCA[:, ci, 0:1])
                for j in range(1, 4):
                    K.stt(yc[:, h, :], zb[:, h, j * sh:j * sh + TT], WCA[:, ci, j:j + 1], yc[:, h, :], ALU.mult, ALU.add)
                hgs()

        def d_tail(pi):
            nm, dst = PIECES[pi]
            yc = YC4[pi % 2]
            if nm == 'va':
                K.act(dst[:, :, :], yc[:, :, :], AF.Silu)
            else:
                K.act(yc[:, :, :], yc[:, :, :], AF.Silu)
                K.act(SQ4[:, :, :], yc[:, :, :], AF.Square)
                for h in range(4):
                    K.mm(P4[:, h * 512:h * 512 + TT], ONESB[:, :], SQ4[:, h, :], True, True)
                p4v = P4[:, :].m(lambda ap: ap.rearrange("p (h t) -> p h t", h=4))[:, :, 0:TT]
                K.act(RS4[:, :, :], p4v, AF.Ln, scale=1.0, bias=EPS)
                K.act(RS4[:, :, :], RS4[:, :, :], AF.Exp, scale=-0.5)
                hgs()
                K.stt(fl(dst[:, :, :]), fl(yc[:, :, :]), (128.0 ** -0.5) if nm == 'qa' else 1.0, fl(RS4[:, :, :]),
                      ALU.mult, ALU.mult)
        d_head(0)
        d_head(1)
        d_tail(0)
        d_head(2)
        d_tail(1)
        d_tail(2)
        wv = wpiece(w_in_v[:, :, OFF['aa']:OFF['aa'] + 520], [128, 8, 520])
        for b in range(NB):
            for k in range(8):
                K.mm(PM[0:BW, b * 8:(b + 1) * 8], HT[:, k, b * BW:(b + 1) * BW], wv[:, k, 0:8], k == 0, k == 7)
        PMv = PM[0:BW, 0:NB * 8].m(lambda ap: ap.rearrange("p (b c) -> p b c", c=8))
        K.act(GA[:, :, :], PMv[:, :, 0:4], AF.Copy)
        K.act(BETA[:, :, :], PMv[:, :, 4:8], AF.Sigmoid)
        K.tt(GA[:, :, :], GA[:, :, :], bc3(DTB[0:BW, :], [BW, NB, 4], 1), ALU.add)
        K.act(GA[:, :, :], GA[:, :, :], AF.Exp)
        K.act(GA[:, :, :], GA[:, :, :], AF.Ln, scale=1.0, bias=1.0)
        K.tt(GG[:, :, :], GA[:, :, :], bc3(NEGA[0:BW, :], [BW, NB, 4], 1), ALU.mult)
        for h in range(4):
            K.act(SOA[:, h, :], fm_chunk(wv, 8 + h * 128), AF.Silu)

        if hg_gen is not None:
            for _ in hg_gen:
                pass
        pa['banks'] = (PA0, PA1)
        A.release(mdin)
        K.tag = kind + ':d_blk'
        NRES = 1 if sample else 2
        RES = []
        for _ in range(NRES):
            RES.append(dict(VB=A.alloc(BF16, [BW, 4, 128]), KGD=A.alloc(BF16, [BW, 4, 128]),
                            QGD=A.alloc(BF16, [128, 4, BW]), ATD=A.alloc(BF16, [BW, 4, BW]),
                            NKC=A.alloc(BF16, [128, 4, BW]), TTR=A.alloc(BF16, [BW, 4, BW]),
                            EGL=A.alloc(F32, [128, 4, NG]), UU=A.alloc(BF16, [BW, 4, 128])))
        NKBG = A.alloc(BF16, [BW, 4, 128])
        LG = A.alloc(F32, [BW, 4, 128]); LB = A.alloc(F32, [BW, 4, 128]); TRIG = A.alloc(F32, [BW, 4, BW])
        EG = A.alloc(F32, [128, 4, BW]); KBT = A.alloc(BF16, [128, 4, BW])
        ET = A.alloc(F32, [BW, 4, BW]); EE = A.alloc(F32, [BW, 4, BW]); IDT = A.alloc(F32, [BW, 4, BW])
        PP = [A.alloc(BF16, [BW, 4, BW]) for _ in range(2)]
        QQ = [A.alloc(BF16, [BW, 4, BW]) for _ in range(2)]
        RR = [A.alloc(BF16, [BW, 4, BW]) for _ in range(2)]
        SC16 = A.alloc(F32, [BW, 16]); SC = SC16[:, 0:8]; NBEG = SC16[:, 8:12]
        QI = A.alloc(BF16, [BW, 4, BW])
        if sample:
            S32 = A.alloc(F32, [128, 64, 128]); SBF = A.alloc(BF16, [128, 64, 128])
            NKM = A.alloc(BF16, [128, 16, 64]); QM = A.alloc(BF16, [128, 16, 64]); UM = A.alloc(BF16, [64, 16, 128])
            K.dma('sp', S32[:, :, :], sd_d.rearrange("s h k v -> k (s h) v"), 'ss')
            K.dma('pool', SBF[:, :, :], sd_d.rearrange("s h k v -> k (s h) v"), 'sq')

        def v4(P, w):
            return P[0:BW, 0:4 * w].m(lambda ap: ap.rearrange("p (h i) -> p h i", h=4))

        def r4(P, w):
            return P[:, 0:4 * w].m(lambda ap: ap.rearrange("p (h i) -> p h i", h=4))

        def build_bc(bb):
            K.tt(LG[:, :, :], ONEF[0:BW, :, :], bc3(GG[:, bb, :], [BW, 4, 128], 2), ALU.mult, eng='pool')
            K.tt(LB[:, :, :], ONEF[0:BW, :, :], bc3(BETA[:, bb, :], [BW, 4, 128], 2), ALU.mult, eng='pool')
            K.tt(TRIG[:, :, :], bc3(TRI, [BW, 4, BW], 1), bc3(GG[:, bb, :], [BW, 4, BW], 2), ALU.mult, eng='pool')

        def prep(b):
            R = RES[b % NRES]
            VB, KGD, QGD, ATD, NKC, TTR, EGL = R['VB'], R['KGD'], R['QGD'], R['ATD'], R['NKC'], R['TTR'], R['EGL']
            bs = slice(b * BW, (b + 1) * BW)
            K.tag = kind + ':d_blk'
            for h in range(4):
                K.tr(PT[0:BW, h * 128:(h + 1) * 128], KT[:, h, bs], IDB[:, :], inc=False)
            for h in range(4):
                K.tr(PT[0:BW, 512 + h * 128:512 + (h + 1) * 128], VT[:, h, bs], IDB[:, :], inc=(h == 3))
            PTk = PT[0:BW, 0:512].m(lambda ap: ap.rearrange("p (h d) -> p h d", h=4))
            PTv = PT[0:BW, 512:1024].m(lambda ap: ap.rearrange("p (h d) -> p h d", h=4))
            K.mm(PM[0:BW, 0:4], TRI, GG[:, b, :], True, True)
            K.mm(PM[0:BW, 4:8], GTM, GG[:, b, :], True, True)
            if b == 0:
                build_bc(0)
            filler()
            yield
            K.tag = kind + ':d_blk'
            K.act(SC[:, :], PM[0:BW, 0:8], AF.Exp)
            K.stt(NBEG[:, :], BETA[:, b, :], -1.0, SC[:, 0:4], ALU.mult, ALU.mult)
            K.tt(VB[:, :, :], PTv, bc3(BETA[:, b, :], [BW, 4, 128], 2), ALU.mult)
            K.tt(NKBG[:, :, :], PTk, bc3(NBEG[:, :], [BW, 4, 128], 2), ALU.mult)
            K.tt(KGD[:, :, :], PTk, bc3(SC[:, 4:8], [BW, 4, 128], 2), ALU.mult)
            for h in range(4):
                K.mm(PC[:, h * BW:(h + 1) * BW], LG[:, h, :], TRI, True, True)
            for h in range(4):
                K.mm(PC2[:, h * BW:(h + 1) * BW], LB[:, h, :], IDF[0:BW, 0:BW], True, True)
            filler()
            yield
            K.tag = kind + ':d_blk'
            K.act(EG[:, :, :], r4(PC, BW), AF.Exp)
            K.tt(KBT[:, :, :], KT[:, :, bs], r4(PC2, BW), ALU.mult)
            if sample:
                K.dve_copy(EGL[:, :, :], EG[:, :, 48:64])
            else:
                K.dve_copy(EGL[:, :, :], EG[:, :, :].m(lambda ap: ap.rearrange("p h (c t) -> p h c t", t=64))[:, :, :, 63])
            K.tt(QGD[:, :, :], QT[:, :, bs], EG[:, :, :], ALU.mult)
            for h in range(4):
                K.mm(PC[0:BW, h * BW:(h + 1) * BW], GTM, TRIG[:, h, :], True, True)
            for h in range(4):
                K.mm(PC2[0:BW, h * BW:(h + 1) * BW], TRIG[:, h, :], GTM, True, True)
            for h in range(4):
                K.mm(PU[0:BW, h * BW:(h + 1) * BW], KT[:, h, bs], QT[:, h, bs], True, True)
            if b + 1 < NB:
                build_bc(b + 1)
            filler()
            yield
            K.tag = kind + ':d_blk'
            K.act(ET[:, :, :], v4(PC, BW), AF.Exp)
            K.act(EE[:, :, :], v4(PC2, BW), AF.Exp)
            K.tt(IDT[:, :, :], ET[:, :, :], MIT4, ALU.mult, eng='pool')
            K.tt(ET[:, :, :], ET[:, :, :], MST4, ALU.mult)
            K.tt(EE[:, :, :], EE[:, :, :], MS4, ALU.mult)
            K.tt(ATD[:, :, :], v4(PU, BW), IDT[:, :, :], ALU.mult)
            for h in range(4):
                K.mm(PC[0:BW, h * BW:(h + 1) * BW], KT[:, h, bs], KBT[:, h, :], True, True)
            for h in range(4):
                K.mm(PC2[0:BW, h * BW:(h + 1) * BW], KBT[:, h, :], KT[:, h, bs], True, True)
            filler()
            yield
            K.tag = kind + ':d_blk'
            K.tt(PP[0][:, :, :], v4(PC, BW), ET[:, :, :], ALU.mult)
            K.tt(QQ[0][:, :, :], v4(PC2, BW), EE[:, :, :], ALU.mult)
            K.ts(RR[0][:, :, :], PP[0][:, :, :], -1.0, None, ALU.mult)
            K.tt(RR[0][:, :, :], RR[0][:, :, :], I4[0:BW, :, 0:BW], ALU.add)
            cur = 0
            for it in range(1, 6):
                K.tag = kind + ':d_inv'
                nxt = 1 - cur
                for h in range(4):
                    K.mm(PC[0:BW, h * BW:(h + 1) * BW], PP[cur][:, h, :], QQ[cur][:, h, :], True, True)
                if it < 5:
                    for h in range(4):
                        K.mm(PC2[0:BW, h * BW:(h + 1) * BW], QQ[cur][:, h, :], PP[cur][:, h, :], True, True)
                filler()
                yield
                K.tag = kind + ':d_inv'
                K.tt(QI[:, :, :], v4(PC, BW), I4[0:BW, :, 0:BW], ALU.add)
                if it < 5:
                    K.dve_copy(QQ[nxt][:, :, :], v4(PC, BW))
                    K.act(PP[nxt][:, :, :], v4(PC2, BW), AF.Copy)
                for h in range(4):
                    K.mm(PU[0:BW, h * BW:(h + 1) * BW], QI[:, h, :], RR[cur][:, h, :], True, True)
                filler()
                yield
                K.tag = kind + ':d_inv'
                K.dve_copy((TTR if it == 5 else RR[nxt])[:, :, :], v4(PU, BW))
                cur = nxt
            for h in range(4):
                K.mm(PC[:, h * BW:(h + 1) * BW], NKBG[:, h, :], TTR[:, h, :], True, True)
            yield
            K.tag = kind + ':d_inv'
            K.act(NKC[:, :, :], r4(PC, BW), AF.Copy)
            yield

        def ser(b):
            R = RES[b % NRES]
            VB, KGD, QGD, ATD, NKC, TTR, EGL, UU = (R['VB'], R['KGD'], R['QGD'], R['ATD'], R['NKC'], R['TTR'],
                                                    R['EGL'], R['UU'])
            bs = slice(b * BW, (b + 1) * BW)
            PUd = PA0[0:BW, :].m(lambda ap: ap.rearrange("p (h d) -> p h d", h=4))
            for ci, (r0, r1) in enumerate(chunks):
                K.tag = kind + ':d_ser'
                for h in range(4):
                    K.mm(PA0[0:BW, h * 128:(h + 1) * 128], TTR[:, h, :], VB[:, h, :], True, False)
                    K.mm(PA0[0:BW, h * 128:(h + 1) * 128], NKC[:, h, :], SDB[:, h, :], False, True)
                filler()
                yield
                K.tag = kind + ':d_ser'
                K.act(UU[r0:r1, :, :], PUd[r0:r1, :, :], AF.Copy)
                for h in range(4):
                    K.mm(PO[:, h * BW + r0:h * BW + r1], SDB[:, h, :], QGD[:, h, r0:r1], True, False)
                    K.mm(PO[:, h * BW + r0:h * BW + r1], UU[r0:r1, h, :], ATD[r0:r1, h, r0:r1], False, True)
                for h in range(4):
                    K.mm(PA0[:, h * 128:(h + 1) * 128], KGD[r0:r1, h, :], UU[r0:r1, h, :], True, True)
                filler()
                yield
                K.tag = kind + ':d_ser'
                for h in range(4):
                    K.stt(SD32[:, h, :], SD32[:, h, :], EGL[:, h, ci:ci + 1], PA0[:, h * 128:(h + 1) * 128],
                          ALU.mult, ALU.add)
                K.act(SDB[:, :, :], SD32[:, :, :], AF.Copy)
                yield
            K.tag = kind + ':d_ser'
            K.act(OA[:, :, bs], PO[:, 0:4 * BW].m(lambda ap: ap.rearrange("p (h i) -> p h i", h=4)), AF.Copy)
            yield

        def step(g):
            if g is None:
                return False
            try:
                next(g)
                return True
            except StopIteration:
                return False

        if not sample:
            stages = [[prep(0)]] + [[ser(b), prep(b + 1)] for b in range(NB - 1)] + [[ser(NB - 1)]]
            for gens in stages:
                while gens:
                    for g in list(gens):
                        if not step(g):
                            gens.remove(g)
                    step(hg_gen)
            while step(hg_gen):
                pass
        else:
            for _ in prep(0):
                pass
            R = RES[0]
            VB, KGD, QGD, ATD, NKC, TTt, EGL, UU = (R['VB'], R['KGD'], R['QGD'], R['ATD'], R['NKC'], R['TTR'],
                                                    R['EGL'], R['UU'])
            PUd = PU[0:BW, :].m(lambda ap: ap.rearrange("p (h d) -> p h d", h=4))
            K.tag = kind + ':d_ser'
            for h in range(4):
                K.tt(NKM[:, :, :], bc3(NKC[:, h, :], [128, 16, 64], 1), SEQC[:, :, :], ALU.mult, eng='pool')
                K.mm(PU[0:64, h * 128:(h + 1) * 128], TTt[:, h, :], VB[:, h, :], True, False)
                for s in range(16):
                    K.mm(PU[0:64, h * 128:(h + 1) * 128], NKM[:, s, :], SBF[:, s * 4 + h, :], False, s == 15)
            K.act(UU[:, :, :], PUd, AF.Copy)
            for h in range(4):
                K.tt(QM[:, :, :], bc3(QGD[:, h, :], [128, 16, 64], 1), SEQC[:, :, :], ALU.mult, eng='pool')
                for s in range(16):
                    K.mm(PO[:, h * 64:(h + 1) * 64], SBF[:, s * 4 + h, :], QM[:, s, :], s == 0, False)
                K.mm(PO[:, h * 64:(h + 1) * 64], UU[:, h, :], ATD[:, h, :], False, True)
                K.tt(UM[:, :, :], bc3(UU[:, h, :], [64, 16, 128], 1), bc3(SEL, [64, 16, 128], 2), ALU.mult)
                for s4 in range(4):
                    PB_ = (PC2, PC)[s4 % 2]
                    for q in range(4):
                        s = s4 * 4 + q
                        K.mm(PB_[:, q * 128:(q + 1) * 128], KGD[:, h, :], UM[:, s, :], True, True)
                    for q in range(4):
                        s = s4 * 4 + q
                        K.stt(S32[:, s * 4 + h, :], S32[:, s * 4 + h, :], EGL[:, h, s:s + 1],
                              PB_[:, q * 128:(q + 1) * 128], ALU.mult, ALU.add)
            K.act(OA[:, :, :], PO[:, 0:256].m(lambda ap: ap.rearrange("p (h i) -> p h i", h=4)), AF.Copy)
            K.dma('sp', od_s.rearrange("s h k v -> k (s h) v"), S32[:, :, :], 'st')
        A.release(md)
        if not sample:
            A.release(mh)
        else:
            for _ in range(NS2):
                WS2.append(A.alloc(BF16, [128, WSLOT_EL]))
        K.tag = kind + ':st4'

        m4_ = A.mark()
        SQ4 = A.alloc(BF16, [128, 4, TT]); RS4 = A.alloc(F32, [128, 4, TT]); TMP4 = A.alloc(F32, [128, 4, TT])
        ONA = A.alloc(BF16, [128, 4, TT]); ONB = A.alloc(BF16, [128, 4, TT]); SOB = A.alloc(BF16, [128, 4, TT])
        MIX = A.alloc(BF16, [128, 8, TT]); M1s = [A.alloc(F32, [128, TT]) for _ in range(2)]; SGs = [A.alloc(F32, [128, TT]) for _ in range(2)]
        fl = lambda v: v.m(lambda ap: ap.rearrange("p h t -> p (h t)"))
        wv = wpiece(w_in_v[:, :, OFF['ogb']:OFF['ogb'] + 512], [128, 8, 512])
        for h in range(4):
            K.act(SOB[:, h, :], fm_chunk(wv, h * 128), AF.Silu)
        for (O_, SO_, ON_, G_) in ((OA, SOA, ONA, GOA), (OB, SOB, ONB, GOB)):
            K.act(SQ4[:, :, :], O_[:, :, :], AF.Square)
            for h in range(4):
                K.mm(P4[:, h * 512:h * 512 + TT], ONESB[:, :], SQ4[:, h, :], True, True)
            p4v = P4[:, :].m(lambda ap: ap.rearrange("p (h t) -> p h t", h=4))[:, :, 0:TT]
            K.act(RS4[:, :, :], p4v, AF.Ln, scale=1.0 / 128, bias=EPS)
            K.act(RS4[:, :, :], RS4[:, :, :], AF.Exp, scale=-0.5)
            K.stt(fl(TMP4[:, :, :]), fl(O_[:, :, :]), G_, fl(RS4[:, :, :]), ALU.mult, ALU.mult)
            K.tt(ON_[:, :, :], TMP4[:, :, :], SO_[:, :, :], ALU.mult)
        pa['banks'] = (PA0, PA1, PM, PC, PC2, PU)
        wa = wpiece(wba_v, [128, 4, D])
        wga = [wpiece(w_in_v[:, :, OFF['ga'] + i * 512:OFF['ga'] + (i + 1) * 512], [128, 8, 512]) for i in range(2)]
        for j in range(8):
            SG = SGs[j % 2]; M1 = M1s[j % 2]
            K.act(SG[:, :], fm_chunk(wga[j // 4], (j % 4) * 128), AF.Sigmoid)
            P = nextbank()
            for h in range(4):
                K.mm(P[:, 0:TT], wa[:, h, j * 128:(j + 1) * 128], ONA[:, h, :], h == 0, h == 3)
            K.tt(M1[:, :], P[:, 0:TT], SG[:, :], ALU.mult)
            K.dve_copy(MIX[:, j, :], M1[:, :])
        wb = wpiece(wbb_v, [128, 4, D])
        wgb = [wpiece(w_in_v[:, :, OFF['gb'] + i * 512:OFF['gb'] + (i + 1) * 512], [128, 8, 512]) for i in range(2)]
        for j in range(8):
            SG = SGs[j % 2]; M1 = M1s[j % 2]
            K.act(SG[:, :], fm_chunk(wgb[j // 4], (j % 4) * 128), AF.Sigmoid)
            P = nextbank()
            for h in range(4):
                K.mm(P[:, 0:TT], wb[:, h, j * 128:(j + 1) * 128], ONB[:, h, :], h == 0, h == 3)
            K.tt(M1[:, :], P[:, 0:TT], SG[:, :], ALU.mult)
            K.tt(MIX[:, j, :], MIX[:, j, :], M1[:, :], ALU.add)
        wos = [wpiece(wo_v[:, :, n * 512:(n + 1) * 512], [128, 8, 512]) for n in range(2)]
        pa['banks'] = (PM, PC, PC2, PU)
        for b in range(NB + 1):
            if b < NB:
                for n in range(2):
                    P = nextbank()
                    for k in range(8):
                        K.mm(P[0:BW, :], MIX[:, k, b * BW:(b + 1) * BW], wos[n][:, k, :], k == 0, k == 7)
                    K.tt(X[b][:, n * 512:(n + 1) * 512], X[b][:, n * 512:(n + 1) * 512], P[0:BW, :], ALU.add)
            if b >= 1:
                norm_post(b - 1)
            if b < NB:
                norm_pre(b, GFFN)
        A.release(m4_)

        m5 = A.mark()
        K.tag = kind + ':ffn'
        pa['banks'] = (PM, PC, PC2, PU, PA0, PA1)
        GB_ = [A.alloc(F32, [128, HF + TT]) for _ in range(2)]
        YC = [A.alloc(F32, [128, TT]) for _ in range(2)]
        if sample:
            _actt = A.alloc(BF16, [128, NFC, TT])
            ACTT = [_actt[:, c, :] for c in range(NFC)]
        else:
            ACTT = [A.alloc(BF16, [128, TT]) for _ in range(NFC)]
        zi = 0
        for p in range(6):
            ncol = min(512, DFF - p * 512)
            wgp = wpiece(wg_v[:, :, p * 512:p * 512 + ncol], [128, 8, ncol])
            wup = wpiece(wu_v[:, :, p * 512:p * 512 + ncol], [128, 8, ncol])
            for cc in range(ncol // 128):
                c = p * 4 + cc
                ps = fm_chunk(wgp, cc * 128)
                gb = GB_[zi % 2]; yc = YC[zi % 2]; zi += 1
                K.dve_copy(gb[:, 0:HF], FT[:, c, :], eng='pool')
                K.act(gb[:, HF:HF + TT], ps, AF.Copy)
                K.dve_copy(FT[:, c, :], gb[:, TT:TT + HF], eng='pool')
                K.act(yc[:, :], gb[:, 0:TT], AF.Copy, scale=WCF[:, c, 0:1])
                for j in range(1, 3):
                    K.stt(yc[:, :], gb[:, j * sh:j * sh + TT], WCF[:, c, j:j + 1], yc[:, :], ALU.mult, ALU.add)
                K.act(yc[:, :], yc[:, :], AF.Silu)
                pu = fm_chunk(wup, cc * 128)
                K.tt(ACTT[c][:, :], yc[:, :], pu, ALU.mult)
        for p in range(6):
            nch = min(4, NFC - p * 4)
            wdp = wpiece(wd_d[p * 512:p * 512 + nch * 128, :].rearrange("(c p) n -> p c n", p=128), [128, nch, D])
            for b in range(NB):
                for n in range(2):
                    P = ALLB[b * 2 + n]
                    for cc in range(nch):
                        K.mm(P[0:BW, :], ACTT[p * 4 + cc][:, b * BW:(b + 1) * BW], wdp[:, cc, n * 512:(n + 1) * 512],
                             p == 0 and cc == 0, p == 5 and cc == nch - 1, inc=(cc == nch - 1))
        for b in range(NB):
            for n in range(2):
                P = ALLB[b * 2 + n]
                K.tt(X[b][:, n * 512:(n + 1) * 512], X[b][:, n * 512:(n + 1) * 512], P[0:BW, :], ALU.add)
        pa['banks'] = (PA0, PA1)
        K.tag = kind + ':fin'
        YO = [A.alloc(F32, [BW, D]) for _ in range(2)]
        for b in range(NB):
            so = (b % 2) * 4
            K.act(JB[b % 2][0:BW, :], X[b][:, :], AF.Square, accum=ST[0:BW, so:so + 1])
            K.act(ST[0:BW, so + 1:so + 2], ST[0:BW, so:so + 1], AF.Ln, scale=1.0 / D, bias=EPS)
            K.act(ST[0:BW, so + 2:so + 3], ST[0:BW, so + 1:so + 2], AF.Exp, scale=-0.5)
            K.stt(YO[b % 2][:, :], X[b][:, :], ST[0:BW, so + 2:so + 3], GFIN[0:BW, :], ALU.mult, ALU.mult)
            K.dma('sp', y_dst[b * BW:(b + 1) * BW, :], YO[b % 2][:, :], 'sty%d' % (b % 2))
            if (not sample) and t0 + TT < 2048:
                K.dma('sp', X[b][:, :], xp_d[t0 + TT + b * BW:t0 + TT + (b + 1) * BW, :], 'x%d' % b)
                PREF['done'] = True
        if sample:
            K.dma('sp', ocq_s, CT[:, :, :], 'st')
            K.dma('sp', off_s, FT[:, :, :], 'st')
        A.release(m5)
        A.release(mp_)

    def pass_pieces():
        L = []
        for nm in ('qb', 'fb', 'ib', 'qa', 'ka', 'va'):
            L.append((w_in_v[:, :, OFF[nm]:OFF[nm] + 512], [128, 8, 512]))
        L.append((w_in_v[:, :, OFF['aa']:OFF['aa'] + 520], [128, 8, 520]))
        L.append((w_in_v[:, :, OFF['ogb']:OFF['ogb'] + 512], [128, 8, 512]))
        L.append((wba_v, [128, 4, D]))
        for i in range(2):
            L.append((w_in_v[:, :, OFF['ga'] + i * 512:OFF['ga'] + (i + 1) * 512], [128, 8, 512]))
        L.append((wbb_v, [128, 4, D]))
        for i in range(2):
            L.append((w_in_v[:, :, OFF['gb'] + i * 512:OFF['gb'] + (i + 1) * 512], [128, 8, 512]))
        for n in range(2):
            L.append((wo_v[:, :, n * 512:(n + 1) * 512], [128, 8, 512]))
        for p in range(6):
            ncol = min(512, DFF - p * 512)
            L.append((wg_v[:, :, p * 512:p * 512 + ncol], [128, 8, ncol]))
            L.append((wu_v[:, :, p * 512:p * 512 + ncol], [128, 8, ncol]))
        for p in range(6):
            nch = min(4, NFC - p * 4)
            L.append((wd_d[p * 512:p * 512 + nch * 128, :].rearrange("(c p) n -> p c n", p=128), [128, nch, D]))
        return L
    for _ in range(5):
        plist.extend(pass_pieces())
    for t in range(4):
        run_pass('P', t * 512)
    K.dma('sp', ocq_p, CTP[:, :, :], 'st')
    K.dma('sp', off_p, FTP[:, :, :], 'st')
    K.dma('sp', od_p.rearrange("h k v -> k h v"), SD32[:, :, :], 'st')
    K.dma('sp', oh_p.rearrange("h k v -> k h v"), SH32[:, :, :], 'st')
    run_pass('S', 0)
    K.finish()
    print("arena peak", A.peak, "inst", K.ninst, "cnt", K.cnt, "standalone waits", K.nwait, "attached", K.nattach)
    _NC_CACHE["K"] = K
    return nc


def kernel(x_prompt, x_sample, cache_conv_qkv, state_delta, state_hgrn, cache_ffn_conv,
           g_attn, w_in, w_conv_a, a_log, dt_bias, g_out_a, w_branch_a, lb_logits,
           g_out_b, w_branch_b, w_out, g_ffn, w_ffn_gate, w_ffn_up, w_ffn_conv,
           w_ffn_down, g_final):
    f = lambda a: np.ascontiguousarray(np.asarray(a, dtype=np.float32))
    consts = _consts()
    shared = dict(
        g_attn=f(g_attn).reshape(1, D), g_ffn=f(g_ffn).reshape(1, D), g_final=f(g_final).reshape(1, D),
        w_in=f(w_in[0]),
        w_conv_a=f(np.asarray(w_conv_a[0]).T.reshape(12, 128, 4).transpose(1, 0, 2)),
        w_ffn_conv=f(np.asarray(w_ffn_conv[0]).T.reshape(NFC, 128, 3).transpose(1, 0, 2)),
        a_log=f(a_log).reshape(1, 4), dt_bias=f(dt_bias).reshape(1, 4),
        g_out_a=f(g_out_a).reshape(128, 1), g_out_b=f(g_out_b).reshape(128, 1),
        w_branch_a=f(w_branch_a[0]), w_branch_b=f(w_branch_b[0]), w_out=f(w_out[0]),
        w_ffn_gate=f(w_ffn_gate[0]), w_ffn_up=f(w_ffn_up[0]), w_ffn_down=f(w_ffn_down[0]),
        lb_t=f(lb_logits), lb_f=f(np.asarray(lb_logits).reshape(2, 4, 128).transpose(2, 0, 1)),
        **consts)
    in_maps = []
    for c in range(NCORES):
        sl = slice(16 * c, 16 * c + 16)
        m = dict(shared)
        m["xp"] = f(x_prompt[c])
        m["xs"] = f(np.asarray(x_sample[sl]).transpose(1, 0, 2).reshape(64, D))
        cq = np.asarray(cache_conv_qkv[0, sl])
        m["cq"] = f(cq.transpose(2, 1, 0).reshape(12, 128, 48).transpose(1, 0, 2))
        cf = np.asarray(cache_ffn_conv[0, sl])
        m["cf"] = f(cf.transpose(2, 1, 0).reshape(NFC, 128, 32).transpose(1, 0, 2))
        m["sd"] = f(state_delta[0, sl]); m["sh"] = f(state_hgrn[0, sl])
        in_maps.append(m)
    if "nc" not in _NC_CACHE:
        _NC_CACHE["nc"] = build_nc()
    res = run_bass_kernel_spmd(_NC_CACHE["nc"], in_maps, core_ids=list(range(NCORES)))
    R = res.results
    y_prompt = np.stack([R[c]["yp"] for c in range(NCORES)])
    y_sample = np.concatenate([R[c]["ys"].reshape(4, 16, D).transpose(1, 0, 2) for c in range(NCORES)])

    def fm2tok(a, nch, j, s):
        return a.transpose(1, 0, 2).reshape(nch * 128, j, s).transpose(2, 1, 0)
    cq_p = np.stack([fm2tok(R[c]["ocq_p"], 12, 3, 1)[0] for c in range(NCORES)])[None]
    ff_p = np.stack([fm2tok(R[c]["off_p"], NFC, 2, 1)[0] for c in range(NCORES)])[None]
    cq_s = np.concatenate([fm2tok(R[c]["ocq_s"], 12, 3, 16) for c in range(NCORES)])[None]
    ff_s = np.concatenate([fm2tok(R[c]["off_s"], NFC, 2, 16) for c in range(NCORES)])[None]
    d_p = np.stack([R[c]["od_p"] for c in range(NCORES)])[None]
    h_p = np.stack([R[c]["oh_p"] for c in range(NCORES)])[None]
    d_s = np.concatenate([R[c]["od_s"] for c in range(NCORES)])[None]
    h_s = np.concatenate([R[c]["oh_s"] for c in range(NCORES)])[None]
    outs = (y_prompt, y_sample, cq_p, d_p, h_p, ff_p, cq_s, d_s, h_s, ff_s)
    return tuple(np.ascontiguousarray(o, dtype=np.float32) for o in outs)
```

```python
import numpy as np
import concourse.bass as bass
import concourse.mybir as mybir

F32 = mybir.dt.float32
BF16 = mybir.dt.bfloat16
AF = mybir.ActivationFunctionType
ALU = mybir.AluOpType
SAME_ENG_SYNC = True
ATTACH_WAIT = True
TRANSITIVE = True
ATTACH_PE = True
NO_SAME_SYNC = ('pe', 'dve')


class T:
    def __init__(self, name, handle):
        self.name = name
        self.h = handle
        self.w = None
        self.r = {}

    def __getitem__(self, idx):
        return V(self, self.h.ap()[idx] if not isinstance(self.h, bass.AP) else self.h[idx])


class V:
    def __init__(self, tiles, ap):
        self.ts = tiles if isinstance(tiles, list) else [tiles]
        self.ap = ap

    def m(self, f):
        return V(self.ts, f(self.ap))

    def __getitem__(self, idx):
        return V(self.ts, self.ap[idx])


class Buf:
    def __init__(self, pages, ap):
        self.ts = pages
        self.ap = ap

    def __getitem__(self, idx):
        return V(self.ts, self.ap[idx])

    def m(self, f):
        return V(self.ts, f(self.ap))


PAGE = 1024


class Arena:
    def __init__(self, K, nbytes):
        self.K = K
        self.words = nbytes // 4
        self.h = K.nc.alloc_sbuf_tensor("arena", [128, self.words], F32)
        self.pages = [T("pg%d" % i, None) for i in range((nbytes + PAGE - 1) // PAGE)]
        self.top = 0
        self.peak = 0

    def alloc(self, dtype, shape):
        esz = 4 if dtype == F32 else 2
        nel = int(np.prod(shape[1:]))
        nb = nel * esz
        nbr = (nb + PAGE - 1) // PAGE * PAGE
        off = self.top
        self.top += nbr
        self.peak = max(self.peak, self.top)
        assert self.top <= self.words * 4, ("arena overflow", self.top)
        w0 = off // 4
        w1 = w0 + (nb + 3) // 4
        ap = self.h.ap()[0:shape[0], w0:w1]
        if dtype != F32:
            ap = ap.bitcast(dtype)
            ap = ap[:, 0:nel]
        if len(shape) == 3:
            ap = ap.rearrange("p (a b) -> p a b", a=shape[1])
        elif len(shape) == 4:
            ap = ap.rearrange("p (a b c) -> p a b c", a=shape[1], b=shape[2])
        return Buf(self.pages[off // PAGE:(off + nbr) // PAGE], ap)

    def mark(self):
        return self.top

    def release(self, m):
        self.top = m


def _ap(x):
    return x.ap if isinstance(x, V) else x


class Trk:
    def __init__(self, nc):
        self.nc = nc
        self.eng = {'pe': nc.tensor, 'act': nc.scalar, 'dve': nc.vector,
                    'pool': nc.gpsimd, 'sp': nc.sync}
        self.sem = {k: nc.alloc_semaphore('s_' + k) for k in self.eng}
        self.cnt = {k: 0 for k in self.eng}
        self.pend = {k: False for k in self.eng}
        self.seen = {k: {} for k in self.eng}
        self.dsem = {}
        self.ninst = {k: 0 for k in self.eng}
        self.tag = ''
        self.nwait = {k: 0 for k in self.eng}
        self.nattach = {k: 0 for k in self.eng}
        self.snap = {k: {} for k in self.eng}
        self.log = {k: [] for k in self.eng}

    def sb(self, name, shape, dtype):
        return T(name, self.nc.alloc_sbuf_tensor(name, list(shape), dtype))

    def ps(self, name, shape, dtype):
        return T(name, self.nc.alloc_psum_tensor(name, list(shape), dtype))

    def dsem_new(self, key):
        self.dsem[key] = [self.nc.alloc_semaphore('d_' + key), 0]

    def _waits(self, e, deps, defer=False):
        pending = []
        for key, val in deps.items():
            if key == e and (e in NO_SAME_SYNC or not SAME_ENG_SYNC):
                continue
            if key in self.dsem:
                val = self.dsem[key][1]
            if self.seen[e].get(key, 0) >= val:
                continue
            sem = self.sem[key] if key in self.sem else self.dsem[key][0]
            pending.append((sem, val))
            self.seen[e][key] = val
            if TRANSITIVE and key in self.snap:
                sn = self.snap[key].get(val)
                if sn:
                    for k2, v2 in sn.items():
                        if k2 != e and self.seen[e].get(k2, 0) < v2:
                            self.seen[e][k2] = v2
        last = None
        if defer and ATTACH_WAIT and pending:
            last = pending.pop()
        for sem, val in pending:
            self.eng[e].wait_ge(sem, val)
            self.nwait[e] += 1
        if last is not None:
            self.nattach[e] += 1
        return last

    @staticmethod
    def _add(deps, d):
        if d is None:
            return
        k, v = d
        if deps.get(k, 0) < v:
            deps[k] = v

    def _deps(self, reads, writes):
        deps = {}
        for t in reads:
            self._add(deps, t.w)
        for t in writes:
            self._add(deps, t.w)
            for k, v in t.r.items():
                self._add(deps, (k, v))
        return deps

    def op(self, e, fn, outs, ins, inc=True):
        reads = [t for x in ins if isinstance(x, V) for t in x.ts]
        writes = [t for x in outs if isinstance(x, V) for t in x.ts]
        last = self._waits(e, self._deps(reads, writes), defer=(e != 'pe' or ATTACH_PE))
        ins_ = fn()
        if last is not None:
            ins_._wait_ge(last[0], last[1])
        self.ninst[e] += 1
        self.log[e].append(self.tag)
        if inc:
            self.cnt[e] += 1
            ins_.then_inc(self.sem[e], 1)
            if TRANSITIVE:
                self.snap[e][self.cnt[e]] = dict(self.seen[e])
            me = (e, self.cnt[e])
            self.pend[e] = False
        else:
            me = (e, self.cnt[e] + 1)
            self.pend[e] = True
        for t in reads:
            if t.r.get(e, 0) < me[1]:
                t.r[e] = me[1]
        for t in writes:
            t.w = me
            t.r = {}
        return ins_

    def dma(self, q, out, in_, dkey):
        reads = list(in_.ts) if isinstance(in_, V) else []
        writes = list(out.ts) if isinstance(out, V) else []
        self._waits(q, self._deps(reads, writes))
        ins_ = self.eng[q].dma_start(out=_ap(out), in_=_ap(in_))
        ds = self.dsem[dkey]
        ds[1] += 16
        ins_.then_inc(ds[0], 16)
        me = (dkey, ds[1])
        for t in reads:
            t.r[dkey] = me[1]
        for t in writes:
            t.w = me
            t.r = {}
        return ins_

    def finish(self):
        for key, (sem, val) in self.dsem.items():
            if val > 0 and self.seen['sp'].get(key, 0) < val:
                self.eng['sp'].wait_ge(sem, val)
        for e in ('pe', 'act', 'dve', 'pool'):
            assert not self.pend[e], e
            if self.cnt[e] > 0:
                self.eng['sp'].wait_ge(self.sem[e], self.cnt[e])

    def mm(self, out, lhsT, rhs, start, stop, inc=None):
        if inc is None:
            inc = stop
        return self.op('pe', lambda: self.nc.tensor.matmul(_ap(out), _ap(lhsT), _ap(rhs), start=start, stop=stop),
                       [out], [lhsT, rhs], inc=inc)

    def tr(self, out, in_, ident, inc=True):
        return self.op('pe', lambda: self.nc.tensor.transpose(_ap(out), _ap(in_), _ap(ident)),
                       [out], [in_, ident], inc=inc)

    def act(self, out, in_, func, scale=1.0, bias=0.0, accum=None):
        kw = {}
        if accum is not None:
            kw['accum_out'] = _ap(accum)
        outs = [out] + ([accum] if accum is not None else [])
        ins = [in_] + [x for x in (scale, bias) if isinstance(x, V)]
        return self.op('act', lambda: self.nc.scalar.activation(_ap(out), _ap(in_), func, bias=_ap(bias),
                                                                scale=_ap(scale), **kw), outs, ins)

    def _ve(self, eng):
        return self.nc.vector if eng == 'dve' else self.nc.gpsimd

    def dve_copy(self, out, in_, eng='dve'):
        return self.op(eng, lambda: self._ve(eng).tensor_copy(_ap(out), _ap(in_)), [out], [in_])

    def tt(self, out, in0, in1, op, eng='dve'):
        return self.op(eng, lambda: self._ve(eng).tensor_tensor(_ap(out), _ap(in0), _ap(in1), op), [out], [in0, in1])

    def ts(self, out, in0, s1, s2, op0, op1=None, eng='dve', accum=None):
        ins = [in0] + [x for x in (s1, s2) if isinstance(x, V)]
        kw = {}
        outs = [out]
        if accum is not None:
            kw['accum_out'] = _ap(accum)
            outs.append(accum)
        if op1 is None:
            return self.op(eng, lambda: self._ve(eng).tensor_scalar(_ap(out), _ap(in0), _ap(s1), None, op0, **kw), outs, ins)
        return self.op(eng, lambda: self._ve(eng).tensor_scalar(_ap(out), _ap(in0), _ap(s1), _ap(s2), op0, op1, **kw), outs, ins)

    def stt(self, out, in0, scalar, in1, op0, op1):
        ins = [in0, in1] + ([scalar] if isinstance(scalar, V) else [])
        return self.op('dve', lambda: self.nc.vector.scalar_tensor_tensor(_ap(out), _ap(in0), _ap(scalar), _ap(in1), op0, op1),
                       [out], ins)

    def memset(self, out, val, eng='dve'):
        return self.op(eng, lambda: self._ve(eng).memset(_ap(out), val), [out], [])
from concourse.bass_utils import run_bass_kernel_spmd

D = 1024
HG_IN_DIN = True
NIN = 6152
DFF = 2816
NFC = 22
EPS = 1e-6
OFF = dict(qa=0, ka=512, va=1024, aa=1536, oga=1544, qb=2056, fb=2568, ib=3080, ogb=3592, ga=4104, gb=5128)
NCORES = 8
WSLOT_EL = 8 * 520


def _masks(grp, pos):
    n = len(grp)
    same = grp[:, None] == grp[None, :]
    le = pos[:, None] <= pos[None, :]
    gt = pos[:, None] > pos[None, :]
    tri = (same & le).astype(np.float32)
    gtm = (same & gt).astype(np.float32)
    mst = (same & (pos[None, :] > pos[:, None])).astype(np.float32)
    mit = (same & (pos[None, :] >= pos[:, None])).astype(np.float32)
    ng = int(grp.max()) + 1
    sel = (grp[:, None] == np.arange(ng)[None, :]).astype(np.float32)

    def pad(a):
        o = np.zeros((128,) + a.shape[1:], np.float32)
        o[:a.shape[0]] = a
        return o

    def rep4(a):
        return pad(np.repeat(a[:, None, :], 4, axis=1))
    selp = np.zeros((128, 16), np.float32)
    selp[:n, :ng] = sel
    out = np.zeros((128, 128 * 2 + 512 * 3 + 16), np.float32)
    out[:, 0:n] = pad(tri)
    out[:, 128:128 + n] = pad(gtm)
    w = n
    out[:, 256:256 + 4 * w] = rep4(mst).reshape(128, -1)
    out[:, 768:768 + 4 * w] = rep4(mit).reshape(128, -1)
    out[:, 1280:1280 + 4 * w] = rep4(gtm).reshape(128, -1)
    out[:, 1792:1808] = selp
    return out


def _consts():
    i = np.arange(128)
    mp = _masks(i // 64, i % 64)
    j = np.arange(64)
    ms = _masks(j % 16, j // 16)
    ident = np.eye(128, dtype=np.float32)
    seqcol = np.zeros((128, 16, 64), np.float32)
    for s in range(16):
        seqcol[:, s, s::16] = 1.0
    return dict(mask_p=mp, mask_s=ms, ident=ident, seqcol=seqcol.reshape(128, 1024))


_NC_CACHE = {}


def build_nc():
    nc = bass.Bass("TRN2", target_bir_lowering=False)
    K = Trk(nc)

    def din(name, shape):
        return nc.dram_tensor(name, list(shape), F32, kind="ExternalInput").ap()

    def dout(name, shape):
        return nc.dram_tensor(name, list(shape), F32, kind="ExternalOutput").ap()

    xp_d = din("xp", [2048, D]); xs_d = din("xs", [64, D])
    cq_d = din("cq", [128, 12, 48]); cf_d = din("cf", [128, NFC, 32])
    sd_d = din("sd", [16, 4, 128, 128]); sh_d = din("sh", [16, 4, 128, 128])
    g_attn_d = din("g_attn", [1, D]); g_ffn_d = din("g_ffn", [1, D]); g_fin_d = din("g_final", [1, D])
    w_in_d = din("w_in", [D, NIN]); wca_d = din("w_conv_a", [128, 12, 4]); wcf_d = din("w_ffn_conv", [128, NFC, 3])
    alog_d = din("a_log", [1, 4]); dtb_d = din("dt_bias", [1, 4])
    goa_d = din("g_out_a", [128, 1]); gob_d = din("g_out_b", [128, 1])
    wba_d = din("w_branch_a", [512, D]); wbb_d = din("w_branch_b", [512, D]); wo_d = din("w_out", [D, D])
    wg_d = din("w_ffn_gate", [D, DFF]); wu_d = din("w_ffn_up", [D, DFF]); wd_d = din("w_ffn_down", [DFF, D])
    lbt_d = din("lb_t", [2, 512]); lbf_d = din("lb_f", [128, 2, 4])
    mp_d = din("mask_p", [128, 1808]); ms_d = din("mask_s", [128, 1808])
    id_d = din("ident", [128, 128]); sc_d = din("seqcol", [128, 1024])

    yp_d = dout("yp", [2048, D]); ys_d = dout("ys", [64, D])
    ocq_p = dout("ocq_p", [128, 12, 3]); ocq_s = dout("ocq_s", [128, 12, 48])
    off_p = dout("off_p", [128, NFC, 2]); off_s = dout("off_s", [128, NFC, 32])
    od_p = dout("od_p", [4, 128, 128]); od_s = dout("od_s", [16, 4, 128, 128])
    oh_p = dout("oh_p", [4, 128, 128]); oh_s = dout("oh_s", [16, 4, 128, 128])

    A = Arena(K, 206 * 1024)
    for k in ("c", "cp", "x", "st", "sty0", "sty1", "ss", "sq", "x0", "x1", "x2", "x3"):
        K.dsem_new(k)
    NSLOT = 5
    for i in range(NSLOT + 8):
        K.dsem_new("w%d" % i)
        K.dsem_new("wst%d" % i)

    psum_h = nc.alloc_psum_tensor("psum_all", [128, 8 * 512], F32)
    bank_t = [T("bank%d" % i, None) for i in range(8)]

    class Bank:
        def __init__(self, i, n=1, dtype=F32):
            self.i = i
            self.n = n
            self.dtype = dtype

        def _ap(self):
            ap = psum_h.ap()[:, self.i * 512:(self.i + self.n) * 512]
            if self.dtype != F32:
                ap = ap.bitcast(self.dtype)
            return ap

        def __getitem__(self, idx):
            return V(bank_t[self.i:self.i + self.n], self._ap()[idx])
    PA0, PA1, PM, PC, PC2, PU, PO = [Bank(i) for i in range(7)]
    PT = Bank(7, 1, BF16)
    PT32 = Bank(7)
    P4 = Bank(3, 4)
    ALLB = [PA0, PA1, PM, PC, PC2, PU, PO, PT32]

    IDF = A.alloc(F32, [128, 128]); IDB = A.alloc(BF16, [128, 128]); ONESB = A.alloc(BF16, [128, 128])
    ONEF = A.alloc(F32, [128, 4, 128]); I4 = A.alloc(F32, [128, 4, 128])
    MK = A.alloc(F32, [128, 1808])
    GATT = A.alloc(F32, [128, D]); GFFN = A.alloc(F32, [128, D]); GFIN = A.alloc(F32, [128, D])
    OMLT = A.alloc(F32, [128, 512]); OMLF = A.alloc(F32, [128, 4])
    SMALL = A.alloc(F32, [128, 64])
    WCA = A.alloc(F32, [128, 12, 4]); WCF = A.alloc(F32, [128, NFC, 3])
    CTP = A.alloc(F32, [128, 12, 3]); FTP = A.alloc(F32, [128, NFC, 2])
    SD32 = A.alloc(F32, [128, 4, 128]); SH32 = A.alloc(F32, [128, 4, 128])
    SDB = A.alloc(BF16, [128, 4, 128]); SHB = A.alloc(BF16, [128, 4, 128])
    WS = [A.alloc(BF16, [128, WSLOT_EL]) for _ in range(NSLOT)]
    wstate = {"i": 0}

    LOOK = 2
    plist = []

    wscr = {}

    def _emit_piece(n):
        dram_ap, shape = plist[n]
        nel = int(np.prod(shape[1:]))
        NPP = len(plist) // 5
        if n >= 4 * NPP + SW0 and WS2:
            ring = WS + WS2
            i = (n - (4 * NPP + SW0)) % len(ring)
            v2 = ring[i][:, 0:nel]
        else:
            i = n % NSLOT
            v2 = WS[i][:, 0:nel]
        v = v2
        if len(shape) == 3:
            v = v2.m(lambda ap: ap.rearrange("p (a b) -> p a b", a=shape[1]))
        NPP = len(plist) // 5
        j = n % NPP
        if n < NPP:
            K.dma('pool', v, dram_ap, "w%d" % i)
            sc = nc.dram_tensor("wsc%d" % j, [128, nel], BF16).ap()
            wscr[j] = V(T("wsc%d" % j, None), sc)
            K.dma('sp', wscr[j], v2, "wst%d" % i)
        else:
            K.dma('pool', v2, wscr[j], "w%d" % i)
        return v

    wviews = {}
    WS2 = []
    NS2 = 8
    SW0 = 7

    def wpiece(dram_ap, shape):
        n = wstate["i"]
        wstate["i"] += 1
        assert plist[n][1] == shape, (n, plist[n][1], shape)
        look = LOOK
        sw = 4 * (len(plist) // 5) + SW0
        if WS2 and n >= sw:
            look = NSLOT + NS2 - 3
        for m in range(n, min(n + look + 1, len(plist))):
            if NS2 > 0 and m >= sw and not WS2:
                break
            if m not in wviews:
                wviews[m] = _emit_piece(m)
        return wviews[m if False else n]

    K.dma('sp', IDF[:, :], id_d, 'c')
    K.dma('pool', IDB[:, :], id_d, 'cp')
    K.memset(ONESB[:, :], 1.0)
    K.memset(ONEF[:, :, :], 1.0)
    for h in range(4):
        K.dma('sp', I4[:, h, :], id_d, 'c')
    K.dma('sp', GATT[:, :], g_attn_d.partition_broadcast(128), 'c')
    K.dma('sp', GFFN[:, :], g_ffn_d.partition_broadcast(128), 'c')
    K.dma('sp', GFIN[:, :], g_fin_d.partition_broadcast(128), 'c')
    K.dma('sp', WCA[:, :, :], wca_d, 'c')
    K.dma('sp', WCF[:, :, :], wcf_d, 'c')
    K.dma('sp', SMALL[:, 0:4], dtb_d.partition_broadcast(128), 'c')
    K.dma('sp', SMALL[:, 4:8], alog_d.partition_broadcast(128), 'c')
    K.dma('sp', SMALL[:, 8:9], goa_d, 'c')
    K.dma('sp', SMALL[:, 9:10], gob_d, 'c')
    m0 = A.mark()
    LBT = A.alloc(F32, [128, 2, 512]); LBF = A.alloc(F32, [128, 2, 4])
    K.dma('sp', LBT[:, :, :], lbt_d.partition_broadcast(128), 'c')
    K.dma('sp', LBF[:, :, :], lbf_d, 'c')
    K.tt(LBT[:, 1, :], LBT[:, 1, :], LBT[:, 0, :], ALU.subtract)
    K.act(OMLT[:, :], LBT[:, 1, :], AF.Sigmoid)
    K.tt(LBF[:, 1, :], LBF[:, 1, :], LBF[:, 0, :], ALU.subtract)
    K.act(OMLF[:, :], LBF[:, 1, :], AF.Sigmoid)
    K.act(SMALL[:, 4:8], SMALL[:, 4:8], AF.Exp)
    K.ts(SMALL[:, 4:8], SMALL[:, 4:8], -1.0, None, ALU.mult)
    A.release(m0)
    K.memset(CTP[:, :, :], 0.0); K.memset(FTP[:, :, :], 0.0)
    K.memset(SD32[:, :, :], 0.0); K.memset(SH32[:, :, :], 0.0)
    K.memset(SDB[:, :, :], 0.0); K.memset(SHB[:, :, :], 0.0)
    DTB = SMALL[:, 0:4]; NEGA = SMALL[:, 4:8]; GOA = SMALL[:, 8:9]; GOB = SMALL[:, 9:10]

    w_in_v = w_in_d.rearrange("(k p) n -> p k n", p=128)
    wg_v = wg_d.rearrange("(k p) n -> p k n", p=128)
    wu_v = wu_d.rearrange("(k p) n -> p k n", p=128)
    wo_v = wo_d.rearrange("(k p) n -> p k n", p=128)
    wba_v = wba_d.rearrange("(k p) n -> p k n", p=128)
    wbb_v = wbb_d.rearrange("(k p) n -> p k n", p=128)

    FILL = {"n": 0}

    def filler(k=None):
        k = FILL["n"] if k is None else k
        for _ in range(k):
            nc.tensor.matmul(psum_h.ap()[:, 2 * 512 + 64:2 * 512 + 192], IDB[:, :].ap, IDB[:, :].ap, start=True, stop=True)

    def bc3(v, shape, axis):
        return v.m(lambda ap: ap.unsqueeze(axis).to_broadcast(shape))

    PREF = {}

    def run_pass(kind, t0):
        sample = kind == 'S'
        TT = 64 if sample else 512
        BW = 64 if sample else 128
        NB = TT // BW
        sh = 16 if sample else 1
        HQ = 3 * sh
        HF = 2 * sh
        NG = 16 if sample else 2
        chunks = [(0, 64)] if sample else [(0, 64), (64, 128)]
        x_src = xs_d if sample else xp_d[t0:t0 + TT, :]
        y_dst = ys_d if sample else yp_d[t0:t0 + TT, :]
        K.dma('sp', MK[:, :], ms_d if sample else mp_d, 'c')
        TRI = MK[0:BW, 0:BW]; GTM = MK[0:BW, 128:128 + BW]

        def m4(c0):
            return MK[0:BW, c0:c0 + 4 * BW].m(lambda ap: ap.rearrange("p (h i) -> p h i", h=4))
        MST4 = m4(256); MIT4 = m4(768); MS4 = m4(1280)
        SEL = MK[0:BW, 1792:1792 + NG]
        mp_ = A.mark()
        X = [A.alloc(F32, [BW, D]) for _ in range(NB)]
        HT = A.alloc(BF16, [128, 8, TT])
        HN = A.alloc(BF16, [BW, D]); JUNK = HN; ST = A.alloc(F32, [128, 8])
        OA = A.alloc(F32, [128, 4, TT]); OB = A.alloc(F32, [128, 4, TT])
        SOA = A.alloc(BF16, [128, 4, TT])
        if sample:
            CT = A.alloc(F32, [128, 12, HQ]); FT = A.alloc(F32, [128, NFC, HF])
            K.dma('sp', CT[:, :, :], cq_d, 'x'); K.dma('sp', FT[:, :, :], cf_d, 'x')
            SEQC = A.alloc(F32, [128, 16, 64])
            K.dma('sp', SEQC[:, :, :], sc_d.rearrange("p (s i) -> p s i", s=16), 'x')
        else:
            CT = CTP; FT = FTP
        if not PREF.get('done'):
            for b in range(NB):
                K.dma('sp', X[b][:, :], x_src[b * BW:(b + 1) * BW, :], 'x%d' % b)
        PREF['done'] = False

        JB = [Bank(0, 2), Bank(0, 2)]
        TB = [PT, Bank(6, 1, BF16)]

        def norm_to_HT(grow):
            for b in range(NB):
                so = (b % 2) * 4
                K.act(JB[b % 2][0:BW, :], X[b][:, :], AF.Square, accum=ST[0:BW, so:so + 1])
                K.act(ST[0:BW, so + 1:so + 2], ST[0:BW, so:so + 1], AF.Ln, scale=1.0 / D, bias=EPS)
                K.act(ST[0:BW, so + 2:so + 3], ST[0:BW, so + 1:so + 2], AF.Exp, scale=-0.5)
                K.stt(HN[:, :], X[b][:, :], ST[0:BW, so + 2:so + 3], grow[0:BW, :], ALU.mult, ALU.mult)
                tb = TB[b % 2]
                for k in range(8):
                    K.tr(tb[:, k * BW:(k + 1) * BW], HN[:, k * 128:(k + 1) * 128], IDB[0:BW, 0:BW], inc=(k == 7))
                K.dve_copy(HT[:, :, b * BW:(b + 1) * BW],
                           tb[:, 0:8 * BW].m(lambda ap: ap.rearrange("p (k t) -> p k t", k=8)))

        def norm_pre(b, grow):
            so = (b % 2) * 4
            K.act(JB[b % 2][0:BW, :], X[b][:, :], AF.Square, accum=ST[0:BW, so:so + 1])
            K.act(ST[0:BW, so + 1:so + 2], ST[0:BW, so:so + 1], AF.Ln, scale=1.0 / D, bias=EPS)
            K.act(ST[0:BW, so + 2:so + 3], ST[0:BW, so + 1:so + 2], AF.Exp, scale=-0.5)
            K.stt(HN[:, :], X[b][:, :], ST[0:BW, so + 2:so + 3], grow[0:BW, :], ALU.mult, ALU.mult)

        def norm_post(b):
            tb = TB[b % 2]
            for k in range(8):
                K.tr(tb[:, k * BW:(k + 1) * BW], HN[:, k * 128:(k + 1) * 128], IDB[0:BW, 0:BW], inc=(k == 7))
            K.dve_copy(HT[:, :, b * BW:(b + 1) * BW],
                       tb[:, 0:8 * BW].m(lambda ap: ap.rearrange("p (k t) -> p k t", k=8)))

        pa = {"i": 0}

        def nextbank():
            bl = pa.get("banks", (PA0, PA1))
            P = bl[pa["i"] % len(bl)]
            pa["i"] += 1
            return P

        def fm_chunk(wv, c0):
            bl = pa.get("banks", (PA0, PA1))
            P = bl[pa["i"] % len(bl)]
            pa["i"] += 1
            for k in range(8):
                K.mm(P[:, 0:TT], wv[:, k, c0:c0 + 128], HT[:, k, :], k == 0, k == 7)
            return P[:, 0:TT]

        def rsq_fm(dst, src_ps, scale):
            K.act(dst, src_ps, AF.Ln, scale=scale, bias=EPS)
            K.act(dst, dst, AF.Exp, scale=-0.5)

        K.tag = kind + ':norm1'
        norm_to_HT(GATT)

        K.tag = kind + ':hg_in'
        mh = A.mark()
        VTOK = [A.alloc(BF16, [BW, 4, 128]) for _ in range(NB)]
        KG = [A.alloc(BF16, [BW, 4, 128]) for _ in range(NB)]
        QGB = A.alloc(BF16, [128, 4, TT])
        ATB = [A.alloc(BF16, [BW, 4, BW]) for _ in range(NB)]
        EBLC = A.alloc(F32, [128, 4, 2 * NB])
        mh2 = A.mark()
        SQB = A.alloc(F32, [128, 4, TT]); SGF = A.alloc(F32, [128, 4, TT])
        KTOK = [A.alloc(F32, [BW, 512]) for _ in range(NB)]
        LOGF = [A.alloc(F32, [BW, 512]) for _ in range(NB)]
        EBC = A.alloc(F32, [128, 4, TT]); ENB = A.alloc(F32, [128, 4, BW])
        KIB = A.alloc(BF16, [128, 4, TT])
        TMPT = A.alloc(F32, [BW, 512])
        wq = wpiece(w_in_v[:, :, OFF['qb']:OFF['qb'] + 512], [128, 8, 512])
        for h in range(4):
            K.act(SQB[:, h, :], fm_chunk(wq, h * 128), AF.Silu)
        wf = wpiece(w_in_v[:, :, OFF['fb']:OFF['fb'] + 512], [128, 8, 512])
        for h in range(4):
            K.act(SGF[:, h, :], fm_chunk(wf, h * 128), AF.Sigmoid, scale=-1.0)
        TB2 = (PM, PC2)
        for b in range(NB):
            P = TB2[b % 2]
            for k in range(8):
                K.mm(P[0:BW, :], HT[:, k, b * BW:(b + 1) * BW], wf[:, k, :], k == 0, k == 7)
            K.act(KTOK[b][:, :], P[0:BW, :], AF.Sigmoid, scale=-1.0)
            K.tt(KTOK[b][:, :], KTOK[b][:, :], OMLT[0:BW, :], ALU.mult)
        wi = wpiece(w_in_v[:, :, OFF['ib']:OFF['ib'] + 512], [128, 8, 512])
        for b in range(NB):
            P = TB2[b % 2]
            for k in range(8):
                K.mm(P[0:BW, :], HT[:, k, b * BW:(b + 1) * BW], wi[:, k, :], k == 0, k == 7)
            K.act(VTOK[b][:, :, :].m(lambda ap: ap.rearrange("p h v -> p (h v)")), P[0:BW, :], AF.Copy)
        for b in range(NB):
            K.act(LOGF[b][:, :], KTOK[b][:, :], AF.Ln, scale=-1.0, bias=1.0)
        K.tag = kind + ':hg_blk'
        for b in range(NB):
            bs = slice(b * BW, (b + 1) * BW)
            PCv = PC[:, 0:4 * BW].m(lambda ap: ap.rearrange("p (h i) -> p h i", h=4))
            for h in range(4):
                K.mm(PC[:, h * BW:(h + 1) * BW], LOGF[b][:, h * 128:(h + 1) * 128], TRI, True, True)
            K.act(EBC[:, :, bs], PCv, AF.Exp)
            K.act(ENB[:, :, :], PCv, AF.Exp, scale=-1.0)
            K.tt(QGB[:, :, bs], SQB[:, :, bs], EBC[:, :, bs], ALU.mult)
            for h in range(4):
                K.stt(KIB[:, h, bs], SGF[:, h, bs], OMLF[:, h:h + 1], ENB[:, h, :], ALU.mult, ALU.mult)
            K.mm(PM[0:BW, :], GTM, LOGF[b][:, :], True, True)
            K.act(TMPT[:, :], PM[0:BW, :], AF.Exp)
            K.tt(KG[b][:, :, :].m(lambda ap: ap.rearrange("p h v -> p (h v)")), KTOK[b][:, :], TMPT[:, :], ALU.mult)
            PUv = PU[0:BW, 0:4 * BW].m(lambda ap: ap.rearrange("p (h i) -> p h i", h=4))
            for h in range(4):
                K.mm(PU[0:BW, h * BW:(h + 1) * BW], KIB[:, h, bs], QGB[:, h, bs], True, True)
            K.tt(ATB[b][:, :, :], PUv, MIT4, ALU.mult)
        K.tag = kind + ':hg_ser'
        hg_gen = None
        if not sample:
            K.dve_copy(EBLC[:, :, :], EBC[:, :, :].m(lambda ap: ap.rearrange("p h (c t) -> p h c t", t=64))[:, :, :, 63])
            A.release(mh2)

            def hg_serial():
                for b in range(NB):
                    for ci, (r0, r1) in enumerate(chunks):
                        K.tag = kind + ':hg_ser'
                        c0 = b * BW + r0; c1 = b * BW + r1
                        for h in range(4):
                            K.mm(PA1[:, h * BW + r0:h * BW + r1], SHB[:, h, :], QGB[:, h, c0:c1], True, False)
                            K.mm(PA1[:, h * BW + r0:h * BW + r1], VTOK[b][r0:r1, h, :], ATB[b][r0:r1, h, r0:r1], False, True)
                        for h in range(4):
                            K.mm(PA0[:, h * 128:(h + 1) * 128], KG[b][r0:r1, h, :], VTOK[b][r0:r1, h, :], True, True)
                        for h in range(4):
                            K.stt(SH32[:, h, :], SH32[:, h, :], EBLC[:, h, b * 2 + ci:b * 2 + ci + 1],
                                  PA0[:, h * 128:(h + 1) * 128], ALU.mult, ALU.add)
                        K.act(SHB[:, :, :], SH32[:, :, :], AF.Copy)
                        filler()
                        yield
                    K.tag = kind + ':hg_ser'
                    K.act(OB[:, :, b * BW:(b + 1) * BW],
                          PA1[:, 0:4 * BW].m(lambda ap: ap.rearrange("p (h i) -> p h i", h=4)), AF.Copy)
                    yield
            hg_gen = hg_serial()
        else:
            ms_ = A.mark()
            S32 = A.alloc(F32, [128, 64, 128]); SBF = A.alloc(BF16, [128, 64, 128])
            QM = A.alloc(BF16, [128, 16, 64]); VM = A.alloc(BF16, [64, 16, 128]); EBL = A.alloc(F32, [128, 4, 16])
            K.dma('sp', S32[:, :, :], sh_d.rearrange("s h k v -> k (s h) v"), 'ss')
            K.dma('pool', SBF[:, :, :], sh_d.rearrange("s h k v -> k (s h) v"), 'sq')
            for h in range(4):
                K.mm(PM[:, h * 16:(h + 1) * 16], LOGF[0][:, h * 128:(h + 1) * 128], SEL, True, True)
            K.act(EBL[:, :, :], PM[:, 0:64].m(lambda ap: ap.rearrange("p (h s) -> p h s", h=4)), AF.Exp)
            for h in range(4):
                K.tt(QM[:, :, :], bc3(QGB[:, h, :], [128, 16, 64], 1), SEQC[:, :, :], ALU.mult, eng='pool')
                for s in range(16):
                    K.mm(PO[:, h * 64:(h + 1) * 64], SBF[:, s * 4 + h, :], QM[:, s, :], s == 0, False)
                K.mm(PO[:, h * 64:(h + 1) * 64], VTOK[0][:, h, :], ATB[0][:, h, :], False, True)
                K.tt(VM[:, :, :], bc3(VTOK[0][:, h, :], [64, 16, 128], 1),
                     bc3(SEL, [64, 16, 128], 2), ALU.mult)
                for s4 in range(4):
                    PB_ = (PC2, PC)[s4 % 2]
                    for q in range(4):
                        s = s4 * 4 + q
                        K.mm(PB_[:, q * 128:(q + 1) * 128], KG[0][:, h, :], VM[:, s, :], True, True)
                    for q in range(4):
                        s = s4 * 4 + q
                        K.stt(S32[:, s * 4 + h, :], S32[:, s * 4 + h, :], EBL[:, h, s:s + 1],
                              PB_[:, q * 128:(q + 1) * 128], ALU.mult, ALU.add)
            K.act(OB[:, :, :], PO[:, 0:256].m(lambda ap: ap.rearrange("p (h i) -> p h i", h=4)), AF.Copy)
            K.dma('sp', oh_s.rearrange("s h k v -> k (s h) v"), S32[:, :, :], 'st')
            A.release(ms_)
            A.release(mh)

        K.tag = kind + ':d_in'
        pa['banks'] = (PT32, PC, PC2, PU, PO)

        def hgs():
            if hg_gen is not None and HG_IN_DIN:
                try:
                    next(hg_gen)
                except StopIteration:
                    pass
            K.tag = kind + ':d_in'
        md = A.mark()
        QT = A.alloc(BF16, [128, 4, TT]); KT = A.alloc(BF16, [128, 4, TT]); VT = A.alloc(BF16, [128, 4, TT])
        GA = A.alloc(F32, [BW, NB, 4]); GG = A.alloc(F32, [BW, NB, 4]); BETA = A.alloc(F32, [BW, NB, 4])
        mdin = A.mark()
        ZB4 = [A.alloc(F32, [128, 4, HQ + TT]) for _ in range(1)]
        YC4 = [A.alloc(F32, [128, 4, TT]) for _ in range(2)]
        SQ4 = A.alloc(BF16, [128, 4, TT]); RS4 = A.alloc(F32, [128, 4, TT])
        fl = lambda v: v.m(lambda ap: ap.rearrange("p h t -> p (h t)"))
        PIECES = (('qa', QT), ('ka', KT), ('va', VT))

        def d_head(pi):
            nm, dst = PIECES[pi]
            wv = wpiece(w_in_v[:, :, OFF[nm]:OFF[nm] + 512], [128, 8, 512])
            base = {'qa': 0, 'ka': 4, 'va': 8}[nm]
            zb = ZB4[0]; yc = YC4[pi % 2]
            for h in range(4):
                ps = fm_chunk(wv, h * 128)
                ci = base + h
                K.dve_copy(zb[:, h, 0:HQ], CT[:, ci, :], eng='pool')
                K.act(zb[:, h, HQ:HQ + TT], ps, AF.Copy)
                K.dve_copy(CT[:, ci, :], zb[:, h, TT:TT + HQ], eng='pool')
                K.act(yc[:, h, :], zb[:, h, 0:TT], AF.Copy, scale=WCA[:, ci, 0:1])
                for j in range(1, 4):
                    K.stt(yc[:, h, :], zb[:, h, j * sh:j * sh + TT], WCA[:, ci, j:j + 1], yc[:, h, :], ALU.mult, ALU.add)
                hgs()

        def d_tail(pi):
            nm, dst = PIECES[pi]
            yc = YC4[pi % 2]
            if nm == 'va':
                K.act(dst[:, :, :], yc[:, :, :], AF.Silu)
            else:
                K.act(yc[:, :, :], yc[:, :, :], AF.Silu)
                K.act(SQ4[:, :, :], yc[:, :, :], AF.Square)
                for h in range(4):
                    K.mm(P4[:, h * 512:h * 512 + TT], ONESB[:, :], SQ4[:, h, :], True, True)
                p4v = P4[:, :].m(lambda ap: ap.rearrange("p (h t) -> p h t", h=4))[:, :, 0:TT]
                K.act(RS4[:, :, :], p4v, AF.Ln, scale=1.0, bias=EPS)
                K.act(RS4[:, :, :], RS4[:, :, :], AF.Exp, scale=-0.5)
                hgs()
                K.stt(fl(dst[:, :, :]), fl(yc[:, :, :]), (128.0 ** -0.5) if nm == 'qa' else 1.0, fl(RS4[:, :, :]),
                      ALU.mult, ALU.mult)
        d_head(0)
        d_head(1)
        d_tail(0)
        d_head(2)
        d_tail(1)
        d_tail(2)
        wv = wpiece(w_in_v[:, :, OFF['aa']:OFF['aa'] + 520], [128, 8, 520])
        for b in range(NB):
            for k in range(8):
                K.mm(PM[0:BW, b * 8:(b + 1) * 8], HT[:, k, b * BW:(b + 1) * BW], wv[:, k, 0:8], k == 0, k == 7)
        PMv = PM[0:BW, 0:NB * 8].m(lambda ap: ap.rearrange("p (b c) -> p b c", c=8))
        K.act(GA[:, :, :], PMv[:, :, 0:4], AF.Copy)
        K.act(BETA[:, :, :], PMv[:, :, 4:8], AF.Sigmoid)
        K.tt(GA[:, :, :], GA[:, :, :], bc3(DTB[0:BW, :], [BW, NB, 4], 1), ALU.add)
        K.act(GA[:, :, :], GA[:, :, :], AF.Exp)
        K.act(GA[:, :, :], GA[:, :, :], AF.Ln, scale=1.0, bias=1.0)
        K.tt(GG[:, :, :], GA[:, :, :], bc3(NEGA[0:BW, :], [BW, NB, 4], 1), ALU.mult)
        for h in range(4):
            K.act(SOA[:, h, :], fm_chunk(wv, 8 + h * 128), AF.Silu)

        if hg_gen is not None:
            for _ in hg_gen:
                pass
        pa['banks'] = (PA0, PA1)
        A.release(mdin)
        K.tag = kind + ':d_blk'
        NRES = 1 if sample else 2
        RES = []
        for _ in range(NRES):
            RES.append(dict(VB=A.alloc(BF16, [BW, 4, 128]), KGD=A.alloc(BF16, [BW, 4, 128]),
                            QGD=A.alloc(BF16, [128, 4, BW]), ATD=A.alloc(BF16, [BW, 4, BW]),
                            NKC=A.alloc(BF16, [128, 4, BW]), TTR=A.alloc(BF16, [BW, 4, BW]),
                            EGL=A.alloc(F32, [128, 4, NG]), UU=A.alloc(BF16, [BW, 4, 128])))
        NKBG = A.alloc(BF16, [BW, 4, 128])
        LG = A.alloc(F32, [BW, 4, 128]); LB = A.alloc(F32, [BW, 4, 128]); TRIG = A.alloc(F32, [BW, 4, BW])
        EG = A.alloc(F32, [128, 4, BW]); KBT = A.alloc(BF16, [128, 4, BW])
        ET = A.alloc(F32, [BW, 4, BW]); EE = A.alloc(F32, [BW, 4, BW]); IDT = A.alloc(F32, [BW, 4, BW])
        PP = [A.alloc(BF16, [BW, 4, BW]) for _ in range(2)]
        QQ = [A.alloc(BF16, [BW, 4, BW]) for _ in range(2)]
        RR = [A.alloc(BF16, [BW, 4, BW]) for _ in range(2)]
        SC16 = A.alloc(F32, [BW, 16]); SC = SC16[:, 0:8]; NBEG = SC16[:, 8:12]
        QI = A.alloc(BF16, [BW, 4, BW])
        if sample:
            S32 = A.alloc(F32, [128, 64, 128]); SBF = A.alloc(BF16, [128, 64, 128])
            NKM = A.alloc(BF16, [128, 16, 64]); QM = A.alloc(BF16, [128, 16, 64]); UM = A.alloc(BF16, [64, 16, 128])
            K.dma('sp', S32[:, :, :], sd_d.rearrange("s h k v -> k (s h) v"), 'ss')
            K.dma('pool', SBF[:, :, :], sd_d.rearrange("s h k v -> k (s h) v"), 'sq')

        def v4(P, w):
            return P[0:BW, 0:4 * w].m(lambda ap: ap.rearrange("p (h i) -> p h i", h=4))

        def r4(P, w):
            return P[:, 0:4 * w].m(lambda ap: ap.rearrange("p (h i) -> p h i", h=4))

        def build_bc(bb):
            K.tt(LG[:, :, :], ONEF[0:BW, :, :], bc3(GG[:, bb, :], [BW, 4, 128], 2), ALU.mult, eng='pool')
            K.tt(LB[:, :, :], ONEF[0:BW, :, :], bc3(BETA[:, bb, :], [BW, 4, 128], 2), ALU.mult, eng='pool')
            K.tt(TRIG[:, :, :], bc3(TRI, [BW, 4, BW], 1), bc3(GG[:, bb, :], [BW, 4, BW], 2), ALU.mult, eng='pool')

        def prep(b):
            R = RES[b % NRES]
            VB, KGD, QGD, ATD, NKC, TTR, EGL = R['VB'], R['KGD'], R['QGD'], R['ATD'], R['NKC'], R['TTR'], R['EGL']
            bs = slice(b * BW, (b + 1) * BW)
            K.tag = kind + ':d_blk'
            for h in range(4):
                K.tr(PT[0:BW, h * 128:(h + 1) * 128], KT[:, h, bs], IDB[:, :], inc=False)
            for h in range(4):
                K.tr(PT[0:BW, 512 + h * 128:512 + (h + 1) * 128], VT[:, h, bs], IDB[:, :], inc=(h == 3))
            PTk = PT[0:BW, 0:512].m(lambda ap: ap.rearrange("p (h d) -> p h d", h=4))
            PTv = PT[0:BW, 512:1024].m(lambda ap: ap.rearrange("p (h d) -> p h d", h=4))
            K.mm(PM[0:BW, 0:4], TRI, GG[:, b, :], True, True)
            K.mm(PM[0:BW, 4:8], GTM, GG[:, b, :], True, True)
            if b == 0:
                build_bc(0)
            filler()
            yield
            K.tag = kind + ':d_blk'
            K.act(SC[:, :], PM[0:BW, 0:8], AF.Exp)
            K.stt(NBEG[:, :], BETA[:, b, :], -1.0, SC[:, 0:4], ALU.mult, ALU.mult)
            K.tt(VB[:, :, :], PTv, bc3(BETA[:, b, :], [BW, 4, 128], 2), ALU.mult)
            K.tt(NKBG[:, :, :], PTk, bc3(NBEG[:, :], [BW, 4, 128], 2), ALU.mult)
            K.tt(KGD[:, :, :], PTk, bc3(SC[:, 4:8], [BW, 4, 128], 2), ALU.mult)
            for h in range(4):
                K.mm(PC[:, h * BW:(h + 1) * BW], LG[:, h, :], TRI, True, True)
            for h in range(4):
                K.mm(PC2[:, h * BW:(h + 1) * BW], LB[:, h, :], IDF[0:BW, 0:BW], True, True)
            filler()
            yield
            K.tag = kind + ':d_blk'
            K.act(EG[:, :, :], r4(PC, BW), AF.Exp)
            K.tt(KBT[:, :, :], KT[:, :, bs], r4(PC2, BW), ALU.mult)
            if sample:
                K.dve_copy(EGL[:, :, :], EG[:, :, 48:64])
            else:
                K.dve_copy(EGL[:, :, :], EG[:, :, :].m(lambda ap: ap.rearrange("p h (c t) -> p h c t", t=64))[:, :, :, 63])
            K.tt(QGD[:, :, :], QT[:, :, bs], EG[:, :, :], ALU.mult)
            for h in range(4):
                K.mm(PC[0:BW, h * BW:(h + 1) * BW], GTM, TRIG[:, h, :], True, True)
            for h in range(4):
                K.mm(PC2[0:BW, h * BW:(h + 1) * BW], TRIG[:, h, :], GTM, True, True)
            for h in range(4):
                K.mm(PU[0:BW, h * BW:(h + 1) * BW], KT[:, h, bs], QT[:, h, bs], True, True)
            if b + 1 < NB:
                build_bc(b + 1)
            filler()
            yield
            K.tag = kind + ':d_blk'
            K.act(ET[:, :, :], v4(PC, BW), AF.Exp)
            K.act(EE[:, :, :], v4(PC2, BW), AF.Exp)
            K.tt(IDT[:, :, :], ET[:, :, :], MIT4, ALU.mult, eng='pool')
            K.tt(ET[:, :, :], ET[:, :, :], MST4, ALU.mult)
            K.tt(EE[:, :, :], EE[:, :, :], MS4, ALU.mult)
            K.tt(ATD[:, :, :], v4(PU, BW), IDT[:, :, :], ALU.mult)
            for h in range(4):
                K.mm(PC[0:BW, h * BW:(h + 1) * BW], KT[:, h, bs], KBT[:, h, :], True, True)
            for h in range(4):
                K.mm(PC2[0:BW, h * BW:(h + 1) * BW], KBT[:, h, :], KT[:, h, bs], True, True)
            filler()
            yield
            K.tag = kind + ':d_blk'
            K.tt(PP[0][:, :, :], v4(PC, BW), ET[:, :, :], ALU.mult)
            K.tt(QQ[0][:, :, :], v4(PC2, BW), EE[:, :, :], ALU.mult)
            K.ts(RR[0][:, :, :], PP[0][:, :, :], -1.0, None, ALU.mult)
            K.tt(RR[0][:, :, :], RR[0][:, :, :], I4[0:BW, :, 0:BW], ALU.add)
            cur = 0
            for it in range(1, 6):
                K.tag = kind + ':d_inv'
                nxt = 1 - cur
                for h in range(4):
                    K.mm(PC[0:BW, h * BW:(h + 1) * BW], PP[cur][:, h, :], QQ[cur][:, h, :], True, True)
                if it < 5:
                    for h in range(4):
                        K.mm(PC2[0:BW, h * BW:(h + 1) * BW], QQ[cur][:, h, :], PP[cur][:, h, :], True, True)
                filler()
                yield
                K.tag = kind + ':d_inv'
                K.tt(QI[:, :, :], v4(PC, BW), I4[0:BW, :, 0:BW], ALU.add)
                if it < 5:
                    K.dve_copy(QQ[nxt][:, :, :], v4(PC, BW))
                    K.act(PP[nxt][:, :, :], v4(PC2, BW), AF.Copy)
                for h in range(4):
                    K.mm(PU[0:BW, h * BW:(h + 1) * BW], QI[:, h, :], RR[cur][:, h, :], True, True)
                filler()
                yield
                K.tag = kind + ':d_inv'
                K.dve_copy((TTR if it == 5 else RR[nxt])[:, :, :], v4(PU, BW))
                cur = nxt
            for h in range(4):
                K.mm(PC[:, h * BW:(h + 1) * BW], NKBG[:, h, :], TTR[:, h, :], True, True)
            yield
            K.tag = kind + ':d_inv'
            K.act(NKC[:, :, :], r4(PC, BW), AF.Copy)
            yield

        def ser(b):
            R = RES[b % NRES]
            VB, KGD, QGD, ATD, NKC, TTR, EGL, UU = (R['VB'], R['KGD'], R['QGD'], R['ATD'], R['NKC'], R['TTR'],
                                                    R['EGL'], R['UU'])
            bs = slice(b * BW, (b + 1) * BW)
            PUd = PA0[0:BW, :].m(lambda ap: ap.rearrange("p (h d) -> p h d", h=4))
            for ci, (r0, r1) in enumerate(chunks):
                K.tag = kind + ':d_ser'
                for h in range(4):
                    K.mm(PA0[0:BW, h * 128:(h + 1) * 128], TTR[:, h, :], VB[:, h, :], True, False)
                    K.mm(PA0[0:BW, h * 128:(h + 1) * 128], NKC[:, h, :], SDB[:, h, :], False, True)
                filler()
                yield
                K.tag = kind + ':d_ser'
                K.act(UU[r0:r1, :, :], PUd[r0:r1, :, :], AF.Copy)
                for h in range(4):
                    K.mm(PO[:, h * BW + r0:h * BW + r1], SDB[:, h, :], QGD[:, h, r0:r1], True, False)
                    K.mm(PO[:, h * BW + r0:h * BW + r1], UU[r0:r1, h, :], ATD[r0:r1, h, r0:r1], False, True)
                for h in range(4):
                    K.mm(PA0[:, h * 128:(h + 1) * 128], KGD[r0:r1, h, :], UU[r0:r1, h, :], True, True)
                filler()
                yield
                K.tag = kind + ':d_ser'
                for h in range(4):
                    K.stt(SD32[:, h, :], SD32[:, h, :], EGL[:, h, ci:ci + 1], PA0[:, h * 128:(h + 1) * 128],
                          ALU.mult, ALU.add)
                K.act(SDB[:, :, :], SD32[:, :, :], AF.Copy)
                yield
            K.tag = kind + ':d_ser'
            K.act(OA[:, :, bs], PO[:, 0:4 * BW].m(lambda ap: ap.rearrange("p (h i) -> p h i", h=4)), AF.Copy)
            yield

        def step(g):
            if g is None:
                return False
            try:
                next(g)
                return True
            except StopIteration:
                return False

        if not sample:
            stages = [[prep(0)]] + [[ser(b), prep(b + 1)] for b in range(NB - 1)] + [[ser(NB - 1)]]
            for gens in stages:
                while gens:
                    for g in list(gens):
                        if not step(g):
                            gens.remove(g)
                    step(hg_gen)
            while step(hg_gen):
                pass
        else:
            for _ in prep(0):
                pass
            R = RES[0]
            VB, KGD, QGD, ATD, NKC, TTt, EGL, UU = (R['VB'], R['KGD'], R['QGD'], R['ATD'], R['NKC'], R['TTR'],
                                                    R['EGL'], R['UU'])
            PUd = PU[0:BW, :].m(lambda ap: ap.rearrange("p (h d) -> p h d", h=4))
            K.tag = kind + ':d_ser'
            for h in range(4):
                K.tt(NKM[:, :, :], bc3(NKC[:, h, :], [128, 16, 64], 1), SEQC[:, :, :], ALU.mult, eng='pool')
                K.mm(PU[0:64, h * 128:(h + 1) * 128], TTt[:, h, :], VB[:, h, :], True, False)
                for s in range(16):
                    K.mm(PU[0:64, h * 128:(h + 1) * 128], NKM[:, s, :], SBF[:, s * 4 + h, :], False, s == 15)
            K.act(UU[:, :, :], PUd, AF.Copy)
            for h in range(4):
                K.tt(QM[:, :, :], bc3(QGD[:, h, :], [128, 16, 64], 1), SEQC[:, :, :], ALU.mult, eng='pool')
                for s in range(16):
                    K.mm(PO[:, h * 64:(h + 1) * 64], SBF[:, s * 4 + h, :], QM[:, s, :], s == 0, False)
                K.mm(PO[:, h * 64:(h + 1) * 64], UU[:, h, :], ATD[:, h, :], False, True)
                K.tt(UM[:, :, :], bc3(UU[:, h, :], [64, 16, 128], 1), bc3(SEL, [64, 16, 128], 2), ALU.mult)
                for s4 in range(4):
                    PB_ = (PC2, PC)[s4 % 2]
                    for q in range(4):
                        s = s4 * 4 + q
                        K.mm(PB_[:, q * 128:(q + 1) * 128], KGD[:, h, :], UM[:, s, :], True, True)
                    for q in range(4):
                        s = s4 * 4 + q
                        K.stt(S32[:, s * 4 + h, :], S32[:, s * 4 + h, :], EGL[:, h, s:s + 1],
                              PB_[:, q * 128:(q + 1) * 128], ALU.mult, ALU.add)
            K.act(OA[:, :, :], PO[:, 0:256].m(lambda ap: ap.rearrange("p (h i) -> p h i", h=4)), AF.Copy)
            K.dma('sp', od_s.rearrange("s h k v -> k (s h) v"), S32[:, :, :], 'st')
        A.release(md)
        if not sample:
            A.release(mh)
        else:
            for _ in range(NS2):
                WS2.append(A.alloc(BF16, [128, WSLOT_EL]))
        K.tag = kind + ':st4'

        m4_ = A.mark()
        SQ4 = A.alloc(BF16, [128, 4, TT]); RS4 = A.alloc(F32, [128, 4, TT]); TMP4 = A.alloc(F32, [128, 4, TT])
        ONA = A.alloc(BF16, [128, 4, TT]); ONB = A.alloc(BF16, [128, 4, TT]); SOB = A.alloc(BF16, [128, 4, TT])
        MIX = A.alloc(BF16, [128, 8, TT]); M1s = [A.alloc(F32, [128, TT]) for _ in range(2)]; SGs = [A.alloc(F32, [128, TT]) for _ in range(2)]
        fl = lambda v: v.m(lambda ap: ap.rearrange("p h t -> p (h t)"))
        wv = wpiece(w_in_v[:, :, OFF['ogb']:OFF['ogb'] + 512], [128, 8, 512])
        for bi, (O_, SO_, ON_, G_) in enumerate(((OA, SOA, ONA, GOA), (OB, SOB, ONB, GOB))):
            if bi == 1:
                for h in range(4):
                    K.act(SOB[:, h, :], fm_chunk(wv, h * 128), AF.Silu)
            K.act(SQ4[:, :, :], O_[:, :, :], AF.Square)
            for h in range(4):
                K.mm(P4[:, h * 512:h * 512 + TT], ONESB[:, :], SQ4[:, h, :], True, True)
            p4v = P4[:, :].m(lambda ap: ap.rearrange("p (h t) -> p h t", h=4))[:, :, 0:TT]
            K.act(RS4[:, :, :], p4v, AF.Ln, scale=1.0 / 128, bias=EPS)
            K.act(RS4[:, :, :], RS4[:, :, :], AF.Exp, scale=-0.5)
            K.stt(fl(TMP4[:, :, :]), fl(O_[:, :, :]), G_, fl(RS4[:, :, :]), ALU.mult, ALU.mult)
            K.tt(ON_[:, :, :], TMP4[:, :, :], SO_[:, :, :], ALU.mult)
        pa['banks'] = (PA0, PA1, PM, PC, PC2, PU)
        wa = wpiece(wba_v, [128, 4, D])
        wga = [wpiece(w_in_v[:, :, OFF['ga'] + i * 512:OFF['ga'] + (i + 1) * 512], [128, 8, 512]) for i in range(2)]
        for j in range(8):
            SG = SGs[j % 2]; M1 = M1s[j % 2]
            K.act(SG[:, :], fm_chunk(wga[j // 4], (j % 4) * 128), AF.Sigmoid)
            P = nextbank()
            for h in range(4):
                K.mm(P[:, 0:TT], wa[:, h, j * 128:(j + 1) * 128], ONA[:, h, :], h == 0, h == 3)
            K.tt(M1[:, :], P[:, 0:TT], SG[:, :], ALU.mult)
            K.dve_copy(MIX[:, j, :], M1[:, :])
        wb = wpiece(wbb_v, [128, 4, D])
        wgb = [wpiece(w_in_v[:, :, OFF['gb'] + i * 512:OFF['gb'] + (i + 1) * 512], [128, 8, 512]) for i in range(2)]
        for j in range(8):
            SG = SGs[j % 2]; M1 = M1s[j % 2]
            K.act(SG[:, :], fm_chunk(wgb[j // 4], (j % 4) * 128), AF.Sigmoid)
            P = nextbank()
            for h in range(4):
                K.mm(P[:, 0:TT], wb[:, h, j * 128:(j + 1) * 128], ONB[:, h, :], h == 0, h == 3)
            K.tt(M1[:, :], P[:, 0:TT], SG[:, :], ALU.mult)
            K.tt(MIX[:, j, :], MIX[:, j, :], M1[:, :], ALU.add)
        wos = [wpiece(wo_v[:, :, n * 512:(n + 1) * 512], [128, 8, 512]) for n in range(2)]
        pa['banks'] = (PM, PC, PC2, PU)
        for b in range(NB + 1):
            if b < NB:
                for n in range(2):
                    P = nextbank()
                    for k in range(8):
                        K.mm(P[0:BW, :], MIX[:, k, b * BW:(b + 1) * BW], wos[n][:, k, :], k == 0, k == 7)
                    K.tt(X[b][:, n * 512:(n + 1) * 512], X[b][:, n * 512:(n + 1) * 512], P[0:BW, :], ALU.add)
            if b >= 1:
                norm_post(b - 1)
            if b < NB:
                norm_pre(b, GFFN)
        A.release(m4_)

        m5 = A.mark()
        K.tag = kind + ':ffn'
        pa['banks'] = (PM, PC, PC2, PU, PA0, PA1)
        GB_ = [A.alloc(F32, [128, HF + TT]) for _ in range(2)]
        YC = [A.alloc(F32, [128, TT]) for _ in range(2)]
        if sample:
            _actt = A.alloc(BF16, [128, NFC, TT])
            ACTT = [_actt[:, c, :] for c in range(NFC)]
        else:
            ACTT = [A.alloc(BF16, [128, TT]) for _ in range(NFC)]
        zi = 0
        for p in range(6):
            ncol = min(512, DFF - p * 512)
            wgp = wpiece(wg_v[:, :, p * 512:p * 512 + ncol], [128, 8, ncol])
            wup = wpiece(wu_v[:, :, p * 512:p * 512 + ncol], [128, 8, ncol])
            for cc in range(ncol // 128):
                c = p * 4 + cc
                ps = fm_chunk(wgp, cc * 128)
                gb = GB_[zi % 2]; yc = YC[zi % 2]; zi += 1
                K.dve_copy(gb[:, 0:HF], FT[:, c, :], eng='pool')
                K.act(gb[:, HF:HF + TT], ps, AF.Copy)
                K.dve_copy(FT[:, c, :], gb[:, TT:TT + HF], eng='pool')
                K.act(yc[:, :], gb[:, 0:TT], AF.Copy, scale=WCF[:, c, 0:1])
                for j in range(1, 3):
                    K.stt(yc[:, :], gb[:, j * sh:j * sh + TT], WCF[:, c, j:j + 1], yc[:, :], ALU.mult, ALU.add)
                K.act(yc[:, :], yc[:, :], AF.Silu)
                pu = fm_chunk(wup, cc * 128)
                K.tt(ACTT[c][:, :], yc[:, :], pu, ALU.mult)
        for p in range(6):
            nch = min(4, NFC - p * 4)
            wdp = wpiece(wd_d[p * 512:p * 512 + nch * 128, :].rearrange("(c p) n -> p c n", p=128), [128, nch, D])
            for b in range(NB):
                for n in range(2):
                    P = ALLB[b * 2 + n]
                    for cc in range(nch):
                        K.mm(P[0:BW, :], ACTT[p * 4 + cc][:, b * BW:(b + 1) * BW], wdp[:, cc, n * 512:(n + 1) * 512],
                             p == 0 and cc == 0, p == 5 and cc == nch - 1, inc=(cc == nch - 1))
        for b in range(NB):
            for n in range(2):
                P = ALLB[b * 2 + n]
                K.tt(X[b][:, n * 512:(n + 1) * 512], X[b][:, n * 512:(n + 1) * 512], P[0:BW, :], ALU.add)
        pa['banks'] = (PA0, PA1)
        K.tag = kind + ':fin'
        YO = [A.alloc(F32, [BW, D]) for _ in range(2)]
        for b in range(NB):
            so = (b % 2) * 4
            K.act(JB[b % 2][0:BW, :], X[b][:, :], AF.Square, accum=ST[0:BW, so:so + 1])
            K.act(ST[0:BW, so + 1:so + 2], ST[0:BW, so:so + 1], AF.Ln, scale=1.0 / D, bias=EPS)
            K.act(ST[0:BW, so + 2:so + 3], ST[0:BW, so + 1:so + 2], AF.Exp, scale=-0.5)
            K.stt(YO[b % 2][:, :], X[b][:, :], ST[0:BW, so + 2:so + 3], GFIN[0:BW, :], ALU.mult, ALU.mult)
            K.dma('sp', y_dst[b * BW:(b + 1) * BW, :], YO[b % 2][:, :], 'sty%d' % (b % 2))
            if (not sample) and t0 + TT < 2048:
                K.dma('sp', X[b][:, :], xp_d[t0 + TT + b * BW:t0 + TT + (b + 1) * BW, :], 'x%d' % b)
                PREF['done'] = True
        if sample:
            K.dma('sp', ocq_s, CT[:, :, :], 'st')
            K.dma('sp', off_s, FT[:, :, :], 'st')
        A.release(m5)
        A.release(mp_)

    def pass_pieces():
        L = []
        for nm in ('qb', 'fb', 'ib', 'qa', 'ka', 'va'):
            L.append((w_in_v[:, :, OFF[nm]:OFF[nm] + 512], [128, 8, 512]))
        L.append((w_in_v[:, :, OFF['aa']:OFF['aa'] + 520], [128, 8, 520]))
        L.append((w_in_v[:, :, OFF['ogb']:OFF['ogb'] + 512], [128, 8, 512]))
        L.append((wba_v, [128, 4, D]))
        for i in range(2):
            L.append((w_in_v[:, :, OFF['ga'] + i * 512:OFF['ga'] + (i + 1) * 512], [128, 8, 512]))
        L.append((wbb_v, [128, 4, D]))
        for i in range(2):
            L.append((w_in_v[:, :, OFF['gb'] + i * 512:OFF['gb'] + (i + 1) * 512], [128, 8, 512]))
        for n in range(2):
            L.append((wo_v[:, :, n * 512:(n + 1) * 512], [128, 8, 512]))
        for p in range(6):
            ncol = min(512, DFF - p * 512)
            L.append((wg_v[:, :, p * 512:p * 512 + ncol], [128, 8, ncol]))
            L.append((wu_v[:, :, p * 512:p * 512 + ncol], [128, 8, ncol]))
        for p in range(6):
            nch = min(4, NFC - p * 4)
            L.append((wd_d[p * 512:p * 512 + nch * 128, :].rearrange("(c p) n -> p c n", p=128), [128, nch, D]))
        return L
    for _ in range(5):
        plist.extend(pass_pieces())
    for t in range(4):
        run_pass('P', t * 512)
    K.dma('sp', ocq_p, CTP[:, :, :], 'st')
    K.dma('sp', off_p, FTP[:, :, :], 'st')
    K.dma('sp', od_p.rearrange("h k v -> k h v"), SD32[:, :, :], 'st')
    K.dma('sp', oh_p.rearrange("h k v -> k h v"), SH32[:, :, :], 'st')
    run_pass('S', 0)
    K.finish()
    print("arena peak", A.peak, "inst", K.ninst, "cnt", K.cnt, "standalone waits", K.nwait, "attached", K.nattach)
    _NC_CACHE["K"] = K
    return nc


def kernel(x_prompt, x_sample, cache_conv_qkv, state_delta, state_hgrn, cache_ffn_conv,
           g_attn, w_in, w_conv_a, a_log, dt_bias, g_out_a, w_branch_a, lb_logits,
           g_out_b, w_branch_b, w_out, g_ffn, w_ffn_gate, w_ffn_up, w_ffn_conv,
           w_ffn_down, g_final):
    f = lambda a: np.ascontiguousarray(np.asarray(a, dtype=np.float32))
    consts = _consts()
    shared = dict(
        g_attn=f(g_attn).reshape(1, D), g_ffn=f(g_ffn).reshape(1, D), g_final=f(g_final).reshape(1, D),
        w_in=f(w_in[0]),
        w_conv_a=f(np.asarray(w_conv_a[0]).T.reshape(12, 128, 4).transpose(1, 0, 2)),
        w_ffn_conv=f(np.asarray(w_ffn_conv[0]).T.reshape(NFC, 128, 3).transpose(1, 0, 2)),
        a_log=f(a_log).reshape(1, 4), dt_bias=f(dt_bias).reshape(1, 4),
        g_out_a=f(g_out_a).reshape(128, 1), g_out_b=f(g_out_b).reshape(128, 1),
        w_branch_a=f(w_branch_a[0]), w_branch_b=f(w_branch_b[0]), w_out=f(w_out[0]),
        w_ffn_gate=f(w_ffn_gate[0]), w_ffn_up=f(w_ffn_up[0]), w_ffn_down=f(w_ffn_down[0]),
        lb_t=f(lb_logits), lb_f=f(np.asarray(lb_logits).reshape(2, 4, 128).transpose(2, 0, 1)),
        **consts)
    in_maps = []
    for c in range(NCORES):
        sl = slice(16 * c, 16 * c + 16)
        m = dict(shared)
        m["xp"] = f(x_prompt[c])
        m["xs"] = f(np.asarray(x_sample[sl]).transpose(1, 0, 2).reshape(64, D))
        cq = np.asarray(cache_conv_qkv[0, sl])
        m["cq"] = f(cq.transpose(2, 1, 0).reshape(12, 128, 48).transpose(1, 0, 2))
        cf = np.asarray(cache_ffn_conv[0, sl])
        m["cf"] = f(cf.transpose(2, 1, 0).reshape(NFC, 128, 32).transpose(1, 0, 2))
        m["sd"] = f(state_delta[0, sl]); m["sh"] = f(state_hgrn[0, sl])
        in_maps.append(m)
    if "nc" not in _NC_CACHE:
        _NC_CACHE["nc"] = build_nc()
    res = run_bass_kernel_spmd(_NC_CACHE["nc"], in_maps, core_ids=list(range(NCORES)))
    R = res.results
    y_prompt = np.stack([R[c]["yp"] for c in range(NCORES)])
    y_sample = np.concatenate([R[c]["ys"].reshape(4, 16, D).transpose(1, 0, 2) for c in range(NCORES)])

    def fm2tok(a, nch, j, s):
        return a.transpose(1, 0, 2).reshape(nch * 128, j, s).transpose(2, 1, 0)
    cq_p = np.stack([fm2tok(R[c]["ocq_p"], 12, 3, 1)[0] for c in range(NCORES)])[None]
    ff_p = np.stack([fm2tok(R[c]["off_p"], NFC, 2, 1)[0] for c in range(NCORES)])[None]
    cq_s = np.concatenate([fm2tok(R[c]["ocq_s"], 12, 3, 16) for c in range(NCORES)])[None]
    ff_s = np.concatenate([fm2tok(R[c]["off_s"], NFC, 2, 16) for c in range(NCORES)])[None]
    d_p = np.stack([R[c]["od_p"] for c in range(NCORES)])[None]
    h_p = np.stack([R[c]["oh_p"] for c in range(NCORES)])[None]
    d_s = np.concatenate([R[c]["od_s"] for c in range(NCORES)])[None]
    h_s = np.concatenate([R[c]["oh_s"] for c in range(NCORES)])[None]
    outs = (y_prompt, y_sample, cq_p, d_p, h_p, ff_p, cq_s, d_s, h_s, ff_s)
    return tuple(np.ascontiguousarray(o, dtype=np.float32) for o in outs)
```
